# Optimizing a Trainium2 kernel written in Bass

```python
import jax, jax.numpy as jnp
from jax import lax
import numpy as np

D_MODEL = 2048
BATCH = 16
SEQ = 256
DEPTH = 4
DEC_BATCH = 4
DEC_SEQ = 1024
PAST_LEN = 256

GRID_W = 64
EPS = 1e-6
ROPE_BASE = 10000.0
Q_BLOCK = 128
D_RNN = D_MODEL // 2
LRU_BLOCKS = 8
LRU_BS = D_RNN // LRU_BLOCKS
CONV_W = 4
LRU_C = 8.0
MLA_HEADS = 8
MLA_NOPE = 128
MLA_ROPE = 64
MLA_V = 128
MLA_Q_RANK = D_MODEL // 4
MLA_KV_RANK = D_MODEL // 8
RET_HEADS = 8
RET_DK = 128
RET_DV = 128
RET_CHUNK = 128
D_FF = 4 * D_MODEL
N_BRANCH = 3

COL_SIZES = (D_RNN, D_RNN, MLA_Q_RANK, MLA_KV_RANK, MLA_ROPE,
             RET_HEADS * RET_DK, RET_HEADS * RET_DK, RET_HEADS * RET_DV, RET_HEADS * RET_DV,
             N_BRANCH * D_MODEL)
N_IN = sum(COL_SIZES)
SPLITS = tuple(sum(COL_SIZES[:i + 1]) for i in range(len(COL_SIZES) - 1))

kernel_name = "hybrid_lru_mla_retention_dit_step"


def rms_norm(x, g):
    xf = x.astype(jnp.float32)
    y = xf * lax.rsqrt(jnp.mean(xf * xf, axis=-1, keepdims=True) + EPS)
    return (y * g.astype(jnp.float32)).astype(x.dtype)


def head_norm(x):
    xf = x.astype(jnp.float32)
    mu = jnp.mean(xf, axis=-1, keepdims=True)
    var = jnp.mean(jnp.square(xf - mu), axis=-1, keepdims=True)
    return (xf - mu) * lax.rsqrt(var + EPS)


def modulate(h, shift, scale):
    return h * (1.0 + scale) + shift


def ada_params(cond, w_mod, b_mod):
    return jnp.split(jax.nn.silu(cond) @ w_mod + b_mod, 6, axis=-1)


def axial_rope(x):
    T = x.shape[1]
    half = x.shape[-1] // 2
    rows = T // GRID_W
    row = jnp.repeat(jnp.arange(rows), GRID_W).astype(jnp.float32)
    col = jnp.tile(jnp.arange(GRID_W), rows).astype(jnp.float32)
    inv = ROPE_BASE ** (-jnp.arange(0, half, 2, dtype=jnp.float32) / half)
    bshape = (1, T) + (1,) * (x.ndim - 3) + (half // 2,)

    def rot(xp, pos):
        ang = (pos[:, None] * inv[None, :]).reshape(bshape)
        cos, sin = jnp.cos(ang), jnp.sin(ang)
        x1, x2 = jnp.split(xp.astype(jnp.float32), 2, axis=-1)
        return jnp.concatenate([x1 * cos - x2 * sin, x2 * cos + x1 * sin], axis=-1)

    return jnp.concatenate([rot(x[..., :half], row), rot(x[..., half:], col)], axis=-1).astype(x.dtype)


def rglru_direction(xb, conv_w, conv_b, wa, ba, wx, bx, lam, h0):
    B, T, _ = xb.shape
    xc = lax.conv_general_dilated(xb, conv_w[:, None, :].astype(xb.dtype), window_strides=(1,),
                                  padding=[(CONV_W - 1, 0)], dimension_numbers=("NWC", "WIO", "NWC"),
                                  feature_group_count=D_RNN) + conv_b
    xr = xc.reshape(B, T, LRU_BLOCKS, LRU_BS)
    r = jax.nn.sigmoid((jnp.einsum("btnc,ncd->btnd", xr, wa).reshape(B, T, D_RNN) + ba).astype(jnp.float32))
    i = jax.nn.sigmoid((jnp.einsum("btnc,ncd->btnd", xr, wx).reshape(B, T, D_RNN) + bx).astype(jnp.float32))
    log_a = -LRU_C * r * jax.nn.softplus(-lam.astype(jnp.float32))
    a = jnp.exp(log_a)
    b = jnp.sqrt(-jnp.expm1(2.0 * log_a)) * (i * xc.astype(jnp.float32))

    def step(h, ab):
        h = ab[0] * h + ab[1]
        return h, h

    h_last, hs = lax.scan(step, h0.astype(jnp.float32), (a.swapaxes(0, 1), b.swapaxes(0, 1)))
    return hs.swapaxes(0, 1).astype(xb.dtype), h_last


def rglru_branch(x_rnn, x_gate, p, h0_f, h0_b):
    dirs = []
    for d, (xd, h0) in enumerate(((x_rnn, h0_f), (x_rnn[:, ::-1], h0_b))):
        dirs.append(rglru_direction(xd, p["lru_conv_w"][d], p["lru_conv_b"][d], p["lru_wa"][d], p["lru_ba"][d],
                                    p["lru_wx"][d], p["lru_bx"][d], p["lru_lam"][d], h0))
    (y_f, h_f), (y_b, h_b) = dirs
    y = (y_f + y_b[:, ::-1]) * jax.nn.gelu(x_gate)
    return y @ p["w_br_lru"], jnp.stack([h_f, h_b], axis=1)


def retention_direction(q, k, v, log_g, s0):
    B, T, H, _ = q.shape
    n_chunks = T // RET_CHUNK

    def chunks(t):
        return t.astype(jnp.float32).reshape(B, n_chunks, RET_CHUNK, H, t.shape[-1]).swapaxes(0, 1)

    pos = jnp.arange(RET_CHUNK, dtype=jnp.float32)
    diff = pos[:, None] - pos[None, :]
    inner_decay = jnp.where(diff >= 0, jnp.exp(log_g[:, None, None] * jnp.maximum(diff, 0.0)), 0.0)
    q_decay = jnp.exp(log_g[None, :] * (pos[:, None] + 1.0))
    k_decay = jnp.exp(log_g[None, :] * (RET_CHUNK - 1.0 - pos[:, None]))
    chunk_decay = jnp.exp(log_g * RET_CHUNK)

    def step(s, qkv):
        qc, kc, vc = qkv
        scores = jnp.einsum("bqhd,bkhd->bhqk", qc, kc) * inner_decay
        o = (jnp.einsum("bhqk,bkhe->bqhe", scores, vc)
             + jnp.einsum("bqhd,bhde->bqhe", qc * q_decay[None, :, :, None], s))
        s = (chunk_decay[None, :, None, None] * s
             + jnp.einsum("bkhd,bkhe->bhde", kc * k_decay[None, :, :, None], vc))
        return s, o

    s_last, o = lax.scan(step, s0.astype(jnp.float32), (chunks(q), chunks(k), chunks(v)))
    return o.swapaxes(0, 1).reshape(B, T, H, v.shape[-1]), s_last


def retention_branch(rq, rk, rv, rg, decay_logit, w_br, s0_f, s0_b, rotate):
    B, T, _ = rq.shape
    q = rq.reshape(B, T, RET_HEADS, RET_DK)
    k = rk.reshape(B, T, RET_HEADS, RET_DK) * (RET_DK ** -0.5)
    v = rv.reshape(B, T, RET_HEADS, RET_DV)
    if rotate:
        q, k = axial_rope(q), axial_rope(k)
    log_g = jax.nn.log_sigmoid(decay_logit.astype(jnp.float32))
    o_f, s_f = retention_direction(q, k, v, log_g[0], s0_f)
    o_b, s_b = retention_direction(q[:, ::-1], k[:, ::-1], v[:, ::-1], log_g[1], s0_b)
    o = head_norm(o_f + o_b[:, ::-1]).reshape(B, T, RET_HEADS * RET_DV).astype(rg.dtype) * jax.nn.silu(rg)
    return o @ w_br, jnp.stack([s_f, s_b], axis=1)


def mla_queries(cq_raw, g_q, w_uq, rotate):
    B, T, _ = cq_raw.shape
    q = (rms_norm(cq_raw, g_q) @ w_uq).reshape(B, T, MLA_HEADS, MLA_NOPE + MLA_ROPE)
    q_nope, q_rope = q[..., :MLA_NOPE], q[..., MLA_NOPE:]
    if rotate:
        q_rope = axial_rope(q_rope)
    return q_nope, q_rope


def mla_attend(q_nope, q_rope, ckv, k_rope, w_ukv, w_br):
    B, T = q_nope.shape[:2]
    S = ckv.shape[1]
    kv = (ckv @ w_ukv).reshape(B, S, MLA_HEADS, MLA_NOPE + MLA_V)
    k_nope, v = kv[..., :MLA_NOPE], kv[..., MLA_NOPE:]
    scale = (MLA_NOPE + MLA_ROPE) ** -0.5
    n_blocks = T // Q_BLOCK

    def blocks(t):
        return t.reshape((B, n_blocks, Q_BLOCK) + t.shape[2:]).swapaxes(0, 1)

    def attend(qs):
        qn, qr = qs
        s = jnp.einsum("bqhd,bkhd->bhqk", qn, k_nope) + jnp.einsum("bqhd,bkd->bhqk", qr, k_rope)
        pr = jax.nn.softmax(s.astype(jnp.float32) * scale, axis=-1).astype(v.dtype)
        return jnp.einsum("bhqk,bkhd->bqhd", pr, v)

    o = lax.map(attend, (blocks(q_nope), blocks(q_rope)))
    o = o.swapaxes(0, 1).reshape(B, T, MLA_HEADS * MLA_V)
    return o @ w_br


def merge_branches(gate_logits, u_lru, u_mla, u_ret, w_out):
    g = jax.nn.sigmoid(gate_logits.astype(jnp.float32)).astype(u_lru.dtype)
    g_lru, g_mla, g_ret = jnp.split(g, N_BRANCH, axis=-1)
    return (g_lru * u_lru + g_mla * u_mla + g_ret * u_ret) @ w_out


def mixer_context(h, p):
    x_rnn, x_gate, cq, ckv_raw, kr, rq, rk, rv, rg, gates = jnp.split(h @ p["w_in"], SPLITS, axis=-1)
    B = h.shape[0]
    h0 = jnp.zeros((B, D_RNN), jnp.float32)
    u_lru, st_lru = rglru_branch(x_rnn, x_gate, p, h0, h0)
    ckv = rms_norm(ckv_raw, p["mla_gkv"])
    qn, qr = mla_queries(cq, p["mla_gq"], p["mla_wuq"], rotate=False)
    u_mla = mla_attend(qn, qr, ckv, kr, p["mla_wukv"], p["w_br_mla"])
    s0 = jnp.zeros((B, RET_HEADS, RET_DK, RET_DV), jnp.float32)
    u_ret, st_ret = retention_branch(rq, rk, rv, rg, p["ret_decay"], p["w_br_ret"], s0, s0, rotate=False)
    return merge_branches(gates, u_lru, u_mla, u_ret, p["w_out"]), ckv, kr, st_lru, st_ret


def mixer_latent(h, p, ckv_ctx, kr_ctx, st_lru, st_ret):
    x_rnn, x_gate, cq, ckv_raw, kr, rq, rk, rv, rg, gates = jnp.split(h @ p["w_in"], SPLITS, axis=-1)
    u_lru, _ = rglru_branch(x_rnn, x_gate, p, st_lru[:, 0], st_lru[:, 1])
    ckv = jnp.concatenate([ckv_ctx.astype(h.dtype), rms_norm(ckv_raw, p["mla_gkv"])], axis=1)
    k_rope = jnp.concatenate([kr_ctx.astype(h.dtype), axial_rope(kr)], axis=1)
    qn, qr = mla_queries(cq, p["mla_gq"], p["mla_wuq"], rotate=True)
    u_mla = mla_attend(qn, qr, ckv, k_rope, p["mla_wukv"], p["w_br_mla"])
    u_ret, _ = retention_branch(rq, rk, rv, rg, p["ret_decay"], p["w_br_ret"], st_ret[:, 0], st_ret[:, 1],
                                rotate=True)
    return merge_branches(gates, u_lru, u_mla, u_ret, p["w_out"])


def ffn_sublayer(x, shift, scale, gate, g_pre, g_post, w1, w2):
    h = modulate(rms_norm(x, g_pre), shift, scale)
    y = jnp.square(jax.nn.relu(h @ w1)) @ w2
    return x + gate * rms_norm(y, g_post)


def setup_inputs(seed: int = 0) -> dict:
    key = jax.random.key(seed)
    ks = iter(jax.random.split(key, 40))
    D = D_MODEL

    def nrm(shape, scale):
        return jax.random.normal(next(ks), shape, jnp.float32) * scale

    x_prompt = nrm((BATCH, SEQ, D), 1.0)
    x_sample = nrm((DEC_BATCH, DEC_SEQ, D), 1.0)
    c = nrm((DEC_BATCH, D), 1.0)
    cache_mla_ckv = nrm((DEC_BATCH, DEPTH, PAST_LEN, MLA_KV_RANK), 1.0)
    cache_mla_krope = nrm((DEC_BATCH, DEPTH, PAST_LEN, MLA_ROPE), 1.0)
    state_lru = nrm((DEC_BATCH, DEPTH, 2, D_RNN), 0.5)
    state_ret = nrm((DEC_BATCH, DEPTH, 2, RET_HEADS, RET_DK, RET_DV), RET_DK ** -0.5)
    c_ctx = nrm((D,), 1.0)
    w_mod = nrm((DEPTH, D, 6 * D), 0.5 * D ** -0.5)
    b_mod = nrm((DEPTH, 6 * D), 0.01)
    g_norm = 1.0 + nrm((DEPTH, 4, D), 0.01)
    w_in = nrm((DEPTH, D, N_IN), D ** -0.5)
    lru_conv_w = nrm((DEPTH, 2, CONV_W, D_RNN), CONV_W ** -0.5)
    lru_conv_b = nrm((DEPTH, 2, D_RNN), 0.01)
    lru_wa = nrm((DEPTH, 2, LRU_BLOCKS, LRU_BS, LRU_BS), LRU_BS ** -0.5)
    lru_ba = nrm((DEPTH, 2, D_RNN), 0.01)
    lru_wx = nrm((DEPTH, 2, LRU_BLOCKS, LRU_BS, LRU_BS), LRU_BS ** -0.5)
    lru_bx = nrm((DEPTH, 2, D_RNN), 0.01)
    a_base = jax.random.uniform(next(ks), (DEPTH, 2, D_RNN), jnp.float32, 0.9, 0.999) ** (1.0 / LRU_C)
    lru_lam = jnp.log(a_base) - jnp.log1p(-a_base)
    mla_gq = 1.0 + nrm((DEPTH, MLA_Q_RANK), 0.01)
    mla_gkv = 1.0 + nrm((DEPTH, MLA_KV_RANK), 0.01)
    mla_wuq = nrm((DEPTH, MLA_Q_RANK, MLA_HEADS * (MLA_NOPE + MLA_ROPE)), MLA_Q_RANK ** -0.5)
    mla_wukv = nrm((DEPTH, MLA_KV_RANK, MLA_HEADS * (MLA_NOPE + MLA_V)), MLA_KV_RANK ** -0.5)
    gamma = 1.0 - 2.0 ** (-5.0 - jnp.arange(RET_HEADS, dtype=jnp.float32))
    ret_decay = jnp.log(gamma) - jnp.log1p(-gamma) + nrm((DEPTH, 2, RET_HEADS), 0.01)
    w_br_lru = nrm((DEPTH, D_RNN, D), D_RNN ** -0.5)
    w_br_mla = nrm((DEPTH, MLA_HEADS * MLA_V, D), (MLA_HEADS * MLA_V) ** -0.5)
    w_br_ret = nrm((DEPTH, RET_HEADS * RET_DV, D), (RET_HEADS * RET_DV) ** -0.5)
    w_out = nrm((DEPTH, D, D), D ** -0.5)
    w_ff1 = nrm((DEPTH, D, D_FF), D ** -0.5)
    w_ff2 = nrm((DEPTH, D_FF, D), D_FF ** -0.5)
    return {"x_prompt": x_prompt, "x_sample": x_sample, "c": c,
            "cache_mla_ckv": cache_mla_ckv, "cache_mla_krope": cache_mla_krope,
            "state_lru": state_lru, "state_ret": state_ret,
            "c_ctx": c_ctx, "w_mod": w_mod, "b_mod": b_mod, "g_norm": g_norm, "w_in": w_in,
            "lru_conv_w": lru_conv_w, "lru_conv_b": lru_conv_b, "lru_wa": lru_wa, "lru_ba": lru_ba,
            "lru_wx": lru_wx, "lru_bx": lru_bx, "lru_lam": lru_lam,
            "mla_gq": mla_gq, "mla_gkv": mla_gkv, "mla_wuq": mla_wuq, "mla_wukv": mla_wukv,
            "ret_decay": ret_decay, "w_br_lru": w_br_lru, "w_br_mla": w_br_mla, "w_br_ret": w_br_ret,
            "w_out": w_out, "w_ff1": w_ff1, "w_ff2": w_ff2}


def reference(x_prompt, x_sample, c, cache_mla_ckv, cache_mla_krope, state_lru, state_ret,
              c_ctx, w_mod, b_mod, g_norm, w_in, lru_conv_w, lru_conv_b, lru_wa, lru_ba, lru_wx, lru_bx,
              lru_lam, mla_gq, mla_gkv, mla_wuq, mla_wukv, ret_decay, w_br_lru, w_br_mla, w_br_ret,
              w_out, w_ff1, w_ff2):
    xp, xs = x_prompt, x_sample
    cond_ctx = c_ctx[None, None, :]
    cond_lat = c[:, None, :]
    ckv_out, kr_out, lru_out, ret_out = [], [], [], []
    for l in range(DEPTH):
        p = {"w_in": w_in[l], "lru_conv_w": lru_conv_w[l], "lru_conv_b": lru_conv_b[l],
             "lru_wa": lru_wa[l], "lru_ba": lru_ba[l], "lru_wx": lru_wx[l], "lru_bx": lru_bx[l],
             "lru_lam": lru_lam[l], "mla_gq": mla_gq[l], "mla_gkv": mla_gkv[l], "mla_wuq": mla_wuq[l],
             "mla_wukv": mla_wukv[l], "ret_decay": ret_decay[l], "w_br_lru": w_br_lru[l],
             "w_br_mla": w_br_mla[l], "w_br_ret": w_br_ret[l], "w_out": w_out[l]}

        sh1, sc1, ga1, sh2, sc2, ga2 = ada_params(cond_ctx, w_mod[l], b_mod[l])
        h = modulate(rms_norm(xp, g_norm[l, 0]), sh1, sc1)
        u, ckv, kr, st_l, st_r = mixer_context(h, p)
        xp = xp + ga1 * rms_norm(u, g_norm[l, 1])
        xp = ffn_sublayer(xp, sh2, sc2, ga2, g_norm[l, 2], g_norm[l, 3], w_ff1[l], w_ff2[l])
        ckv_out.append(ckv)
        kr_out.append(kr)
        lru_out.append(st_l.astype(x_prompt.dtype))
        ret_out.append(st_r.astype(x_prompt.dtype))

        sh1, sc1, ga1, sh2, sc2, ga2 = ada_params(cond_lat, w_mod[l], b_mod[l])
        h = modulate(rms_norm(xs, g_norm[l, 0]), sh1, sc1)
        u = mixer_latent(h, p, cache_mla_ckv[:, l], cache_mla_krope[:, l], state_lru[:, l], state_ret[:, l])
        xs = xs + ga1 * rms_norm(u, g_norm[l, 1])
        xs = ffn_sublayer(xs, sh2, sc2, ga2, g_norm[l, 2], g_norm[l, 3], w_ff1[l], w_ff2[l])

    y_prompt, y_sample = xp, xs
    new_mla_ckv = jnp.stack(ckv_out, axis=1)
    new_mla_krope = jnp.stack(kr_out, axis=1)
    new_state_lru = jnp.stack(lru_out, axis=1)
    new_state_ret = jnp.stack(ret_out, axis=1)
    return (y_prompt, y_sample, new_mla_ckv, new_mla_krope, new_state_lru, new_state_ret)
```

```python
import bisect
from contextlib import ExitStack
import numpy as np
import concourse.bass as bass
import concourse.mybir as mybir
from concourse.bass_utils import run_bass_kernel_spmd

F32 = mybir.dt.float32
BF16 = mybir.dt.bfloat16
AF = mybir.ActivationFunctionType
ALU = mybir.AluOpType
ESZ = {F32: 4, BF16: 2}

D = 2048
T = 1024
TH = 512
DEPTH = 4
N_IN = 13120
EPS = 1e-6
C_CQ, C_CKV, C_KR, C_RQ, C_RK, C_RV, C_RG, C_G = 2048, 2560, 2816, 2880, 3904, 4928, 5952, 6976
SK = 128.0 ** -0.5
LNS = float(np.log(SK))
MLA_SCALE = 192.0 ** -0.5
BIG = 8192.0
LMV = -30000.0

X_OFF = 0
PV_OFF = 16384
PVL = 336
GL_OFF = PV_OFF + PVL
GLW = 32
CSR_OFF = GL_OFF + GLW
CSM_OFF = CSR_OFF + 1024
CB_OFF = CSM_OFF + 1024
ONESF_OFF = CB_OFF + 256
XIB_OFF = ONESF_OFF + 128
ADA_OFF = XIB_OFF + 512
DER_OFF = ADA_OFF + 96
SILB_OFF = DER_OFF + 160
LRUO_OFF = SILB_OFF + 8
PERS_END = 20224
assert LRUO_OFF + 256 <= PERS_END
H_OFF = PERS_END
YBR_OFF = H_OFF + 8192
WB_OFF = YBR_OFF + 12288
SCR_OFF = WB_OFF + 4096
ARENA = 53200
SCR_W = ARENA - SCR_OFF

PV_GN = 0
PV_BMOD = 64
PV_CW = 160
PV_CB = 224
PV_BA = 240
PV_BX = 256
PV_LAM = 272
PV_GQ = 288
PV_GKV = 292
PV_H0 = 294
PV_DEC = 310
GL_COND = 0
GL_CROSS = 16
GL_LM = 17
GL_PIDX = 18
GL_POSF = 19
GL_POSB = 21
GL_PSH = 23
DR_A1, DR_GA1, DR_A2, DR_GA2, DR_KL, DR_K2, DR_LG, DR_NLG, DR_TMP = 0, 16, 32, 48, 64, 80, 96, 112, 128

EBLK = [(128, 640), (640, 896), (896, 1024), (1152, 1280), (1280, 1792), (1792, 2048)]


class _StopOps(Exception):
    pass


class Sched:
    ENGS = ("pe", "dve", "act", "pool", "sp")
    BLK = 1024

    def __init__(self, nc, es, arena_words, n_dma_sems=40):
        self.nc = nc
        self.eng = dict(pe=nc.tensor, dve=nc.vector, act=nc.scalar, pool=nc.gpsimd, sp=nc.sync)
        self.sem = {e: es.enter_context(nc.semaphore("s_" + e)) for e in self.ENGS}
        self.nsig = {e: 0 for e in self.ENGS}
        self.nissued = {e: 0 for e in self.ENGS}
        self.last_inst = {e: None for e in self.ENGS}
        self.last_signaled = {e: True for e in self.ENGS}
        self.sig_idx = {e: [] for e in self.ENGS}
        self.sig_val = {e: [] for e in self.ENGS}
        self.waited = {e: {} for e in self.ENGS}
        self.dsem = [es.enter_context(nc.semaphore("d%d" % i)) for i in range(n_dma_sems)]
        self.dcount = [0] * n_dma_sems
        self.dnext = 0
        self.lastw = {}
        self.rd_c = {}
        self.rd_d = {}
        self.nwaits = 0
        self.arena_t = es.enter_context(nc.sbuf_tensor("arena", [128, arena_words], F32))
        self.psum_t = es.enter_context(nc.psum_tensor("psum", [128, 8, 512], F32))

    def sb(self, word_off, shape, dt=F32, p0=0):
        n = 1
        for s in shape[1:]:
            n *= s
        words = (n * ESZ[dt] + 3) // 4
        assert word_off + words <= ARENA, (word_off, shape)
        v = self.arena_t[p0:p0 + shape[0], word_off:word_off + words]
        if dt != F32:
            v = v.bitcast(dt)
        if len(shape) > 2:
            names = " ".join("d%d" % i for i in range(len(shape) - 1))
            kw = {"d%d" % i: shape[i + 1] for i in range(len(shape) - 1)}
            v = v.rearrange("p (%s) -> p %s" % (names, names), **kw)
        return v

    def bank(self, b, parts=128, dt=F32):
        v = self.psum_t[0:parts, b, :]
        if dt != F32:
            v = v.bitcast(dt)
        return v

    def keys(self, ap):
        t = getattr(ap, "tensor", None)
        name = t.name if t is not None else None
        if name not in ("arena", "psum"):
            return ()
        es_ = ESZ[ap.dtype]
        pat = ap.ap
        pstride = pat[0][0]
        off = ap.offset % pstride
        lo = hi = off
        for st, cnt in pat[1:]:
            d = st * (cnt - 1)
            if d < 0:
                lo += d
            else:
                hi += d
        lo_b = lo * es_
        hi_b = (hi + 1) * es_ - 1
        if name == "psum":
            return [("P", b) for b in range(lo_b // 2048, hi_b // 2048 + 1)]
        return [("S", b) for b in range(lo_b // self.BLK, hi_b // self.BLK + 1)]

    def _resolve(self, tok):
        if tok[0] == "d":
            return ("d", tok[1]), self.dsem[tok[1]], tok[2]
        _, e, idx = tok
        i = bisect.bisect_left(self.sig_idx[e], idx)
        if i == len(self.sig_idx[e]):
            assert not self.last_signaled[e]
            self.last_inst[e].then_inc(self.sem[e], 1)
            self.nsig[e] += 1
            self.sig_idx[e].append(self.nissued[e] - 1)
            self.sig_val[e].append(self.nsig[e])
            self.last_signaled[e] = True
        return ("c", e), self.sem[e], self.sig_val[e][i]

    def _wait(self, eng, tok):
        if tok[0] == "c" and tok[1] == eng and eng == "pe":
            return
        key, sem, val = self._resolve(tok)
        if self.waited[eng].get(key, 0) >= val:
            return
        self.waited[eng][key] = val
        self.eng[eng].wait_ge(sem, val)
        self.nwaits += 1

    def _deps(self, eng, rkeys, wkeys):
        toks = set()
        for k in rkeys:
            t = self.lastw.get(k)
            if t is not None:
                toks.add(t)
            if k[0] == "P":
                for e2, t2 in self.rd_c.get(k, {}).items():
                    if e2 != eng:
                        toks.add(t2)
        for k in wkeys:
            t = self.lastw.get(k)
            if t is not None:
                toks.add(t)
            toks.update(self.rd_c.get(k, {}).values())
            toks.update(self.rd_d.get(k, ()))
        for t in sorted(toks, key=lambda t: (t[0], str(t[1]), t[2])):
            self._wait(eng, t)

    def _record(self, tok, rkeys, wkeys):
        for k in rkeys:
            if tok[0] == "c":
                self.rd_c.setdefault(k, {})[tok[1]] = tok
            else:
                self.rd_d.setdefault(k, []).append(tok)
        for k in wkeys:
            self.lastw[k] = tok
            self.rd_c[k] = {}
            self.rd_d[k] = []

    def _rw(self, outs, ins):
        rk, wk = [], []
        for a in ins:
            rk.extend(self.keys(a))
        for a in outs:
            wk.extend(self.keys(a))
        return rk, wk

    def op(self, eng, fn, outs=(), ins=(), sig=None):
        if getattr(self, 'stop_at', None) is not None:
            if self.opcount == self.stop_at:
                raise _StopOps()
            self.opcount += 1
        if sig is None:
            sig = (eng != "pe")
        rk, wk = self._rw(outs, ins)
        self._deps(eng, rk, wk)
        inst = fn()
        idx = self.nissued[eng]
        self.nissued[eng] += 1
        self.last_inst[eng] = inst
        if sig:
            inst.then_inc(self.sem[eng], 1)
            self.nsig[eng] += 1
            self.sig_idx[eng].append(idx)
            self.sig_val[eng].append(self.nsig[eng])
            self.last_signaled[eng] = True
        else:
            self.last_signaled[eng] = False
        self._record(("c", eng, idx), rk, wk)
        return inst

    def dma(self, q, out, in_, **kw):
        slot = self.dnext
        self.dnext = (self.dnext + 1) % len(self.dsem)
        if self.dcount[slot] > 0:
            self._wait(q, ("d", slot, 16 * self.dcount[slot]))
        rk, wk = self._rw([out], [in_])
        self._deps(q, rk, wk)
        self.dcount[slot] += 1
        tok = ("d", slot, 16 * self.dcount[slot])
        self.eng[q].dma_start(out=out, in_=in_, **kw).then_inc(self.dsem[slot], 16)
        self._record(tok, rk, wk)
        return tok

    def finish(self):
        for slot in range(len(self.dsem)):
            if self.dcount[slot] > 0:
                self._wait("sp", ("d", slot, 16 * self.dcount[slot]))
        for e in self.ENGS:
            if e != "sp" and self.nissued[e] > 0:
                self._wait("sp", ("c", e, self.nissued[e] - 1))

    def mm(self, out, lhsT, rhs, start, stop, sig=None):
        nc = self.nc
        if sig is None:
            sig = bool(stop)
        return self.op("pe", lambda: nc.tensor.matmul(out, lhsT, rhs, start=start, stop=stop),
                       outs=[out], ins=[lhsT, rhs], sig=sig)

    def tr(self, out, in_, ident, sig=False):
        nc = self.nc
        return self.op("pe", lambda: nc.tensor.transpose(out, in_, ident), outs=[out], ins=[in_, ident], sig=sig)

    def act(self, out, in_, func, bias=None, scale=None):
        nc = self.nc
        kw = {}
        ins = [in_]
        if bias is not None:
            kw["bias"] = bias
            if not isinstance(bias, (int, float)):
                ins.append(bias)
        if scale is not None:
            kw["scale"] = scale
            if not isinstance(scale, (int, float)):
                ins.append(scale)
        return self.op("act", lambda: nc.scalar.activation(out, in_, func, **kw), outs=[out], ins=ins)

    def tt(self, out, in0, in1, op, eng="dve"):
        e = self.eng[eng]
        return self.op(eng, lambda: e.tensor_tensor(out, in0, in1, op), outs=[out], ins=[in0, in1])

    def ts(self, out, in0, s1, s2, op0, op1=None, eng="dve"):
        e = self.eng[eng]
        ins = [in0] + [s for s in (s1, s2) if s is not None and not isinstance(s, (int, float))]
        if op1 is None:
            return self.op(eng, lambda: e.tensor_scalar(out, in0, s1, s2, op0), outs=[out], ins=ins)
        return self.op(eng, lambda: e.tensor_scalar(out, in0, s1, s2, op0, op1), outs=[out], ins=ins)

    def stt(self, out, in0, scalar, in1, op0, op1):
        nc = self.nc
        ins = [in0, in1] + ([] if isinstance(scalar, (int, float)) else [scalar])
        return self.op("dve", lambda: nc.vector.scalar_tensor_tensor(out, in0, scalar, in1, op0, op1),
                       outs=[out], ins=ins)

    def copy(self, out, in_, eng="dve"):
        nc = self.nc
        if eng == "act":
            return self.op("act", lambda: nc.scalar.copy(out, in_), outs=[out], ins=[in_])
        e = self.eng[eng]
        return self.op(eng, lambda: e.tensor_copy(out, in_), outs=[out], ins=[in_])

    def recip(self, out, in_):
        nc = self.nc
        return self.op("dve", lambda: nc.vector.reciprocal(out, in_), outs=[out], ins=[in_])

    def memset(self, out, val, eng="dve"):
        e = self.eng[eng]
        return self.op(eng, lambda: e.memset(out, val), outs=[out])

    def scan(self, out, d0, d1, init):
        nc = self.nc
        ins = [d0, d1] + ([] if isinstance(init, (int, float)) else [init])
        return self.op("dve", lambda: nc.vector.tensor_tensor_scan(out, d0, d1, init, ALU.mult, ALU.add),
                       outs=[out], ins=ins)


class Ring:
    def __init__(self, items):
        self.items = list(items)
        self.i = 0

    def next(self):
        v = self.items[self.i % len(self.items)]
        self.i += 1
        return v


class _Stop(Exception):
    pass


def build(nl, dbg_names=(), stop_after=None):
    nc = bass.Bass("TRN2", target_bir_lowering=False)
    dI = lambda name, shape: nc.dram_tensor(name, shape, F32, kind="ExternalInput").ap()
    dO = lambda name, shape: nc.dram_tensor(name, shape, F32, kind="ExternalOutput").ap()
    xT = dI("xT", [D, T])
    pv = dI("pv", [nl, 128, PVL])
    gl = dI("gl", [128, GLW])
    cst = dI("cst", [128, 5, 128])
    cs = dI("cs", [128, 4, 1024])
    xib = dI("xib", [128, 512])
    mq = dI("mq", [4, T])
    mk = dI("mk", [4, 1280])
    cckv = dI("cckv", [nl, 256, 256])
    ckr = dI("ckr", [nl, 64, 256])
    sret = dI("sret", [nl, 2, 8, 128, 128])
    w_mod = dI("w_mod", [nl, D, 6 * D])
    w_in = dI("w_in", [nl, D, N_IN])
    lru_wa = dI("lru_wa", [nl, 2, 8, 128, 128])
    lru_wx = dI("lru_wx", [nl, 2, 8, 128, 128])
    w_uq = dI("mla_wuq", [nl, 512, 1536])
    w_ukv = dI("mla_wukv", [nl, 256, 2048])
    w_br = [dI("w_br_lru", [nl, 1024, D]), dI("w_br_mla", [nl, 1024, D]), dI("w_br_ret", [nl, 1024, D])]
    w_out = dI("w_out", [nl, D, D])
    w_ff1 = dI("w_ff1", [nl, D, 4 * D])
    w_ff2 = dI("w_ff2", [nl, 4 * D, D])
    yT = dO("yT", [D, T])
    ckvo = dO("ckvo", [nl, 256, T])
    kro = dO("kro", [nl, 64, T])
    lruo = dO("lruo", [128, 256])
    reto = dO("reto", [nl, 4, 2, 8, 128, 128])
    dbg_out = {n: dO("dbg_" + n, shp) for n, shp in dbg_names}

    with ExitStack() as es:
        S = Sched(nc, es, ARENA)
        sb = S.sb
        X = sb(X_OFF, [128, 16, T])
        PV = sb(PV_OFF, [128, PVL])
        GL = sb(GL_OFF, [128, GLW])
        CSR = sb(CSR_OFF, [128, 2, T], BF16)
        CSM = sb(CSM_OFF, [128, 2, T], BF16)
        CB = sb(CB_OFF, [128, 4, 128], BF16)
        IDENT, ONES, RRET, RMLA = CB[:, 0, :], CB[:, 1, :], CB[:, 2, :], CB[:, 3, :]
        ONESF = sb(ONESF_OFF, [128, 128])
        XIB = sb(XIB_OFF, [128, 512])
        ADA = sb(ADA_OFF, [128, 96])
        DER = sb(DER_OFF, [128, 160])
        SILB = sb(SILB_OFF, [128, 16], BF16)
        LRUO = sb(LRUO_OFF, [128, 64])
        ADA_NEXT = sb(LRUO_OFF + 64, [128, 96])
        H = sb(H_OFF, [128, 16, T], BF16)
        U = sb(H_OFF, [128, 16, T], F32)
        YBR = sb(YBR_OFF, [128, 3, 8, T], BF16)
        CROSS = GL[:, GL_CROSS:GL_CROSS + 1]
        LM = GL[:, GL_LM:GL_LM + 1]
        PIDX = GL[:, GL_PIDX:GL_PIDX + 1]
        EPSC = DER[:, 159:160]
        ONEC = DER[:, 158:159]
        LNSC = DER[:, 157:158]

        def dbg(name, ap):
            if name in dbg_out:
                S.dma("pool" if ap.dtype != F32 else "sp", dbg_out[name], ap)

        def chk(name):
            if stop_after == name:
                raise _Stop()

        S.dma("sp", GL, gl)
        S.dma("sp", ONESF, cst[:, 4, :])
        S.dma("sp", XIB, xib)
        S.dma("pool", CB, cst[:, 0:4, :])
        S.dma("pool", CSR, cs[:, 0:2, :])
        S.dma("pool", CSM, cs[:, 2:4, :])
        S.dma("sp", X, xT.rearrange("(c p) t -> p c t", p=128))
        S.memset(LRUO, 0.0)
        S.memset(EPSC, EPS)
        S.memset(ONEC, 1.0)
        S.memset(LNSC, LNS)
        sil32 = sb(SCR_OFF, [128, 16])
        S.act(sil32, GL[:, GL_COND:GL_COND + 16], AF.Silu)
        S.copy(SILB, sil32)

        wslot = [0]

        def wbuf(shape, slot=None):
            if slot is None:
                slot = wslot[0] % 2
                wslot[0] += 1
            return sb(WB_OFF + slot * 2048, shape, BF16)

        def wview(w, l):
            return w[l].rearrange("(kc p) n -> p kc n", p=128)

        def rstd_from_bank(bank_ap, out, n_feat, tmp):
            S.act(tmp, bank_ap, AF.Sqrt, bias=EPSC, scale=1.0 / n_feat)
            S.recip(out, tmp)

        def pre_norm(acol, bcol, scr):
            sq = [sb(scr + i * 256, [128, TH], BF16) for i in range(2)]
            rstd = sb(scr + 512, [128, TH])
            tmpr = sb(scr + 1024, [128, TH])
            tm = [sb(scr + 1536 + i * 512, [128, TH]) for i in range(2)]
            for th in range(2):
                sl = slice(th * TH, (th + 1) * TH)
                bk = S.bank(6 + th)
                for fc in range(16):
                    s = sq[fc % 2]
                    S.act(s, X[:, fc, sl], AF.Square)
                    S.mm(bk, ONES, s, start=(fc == 0), stop=(fc == 15))
                rstd_from_bank(bk, rstd, D, tmpr)
                for fc in range(16):
                    t = tm[fc % 2]
                    S.stt(t, X[:, fc, sl], acol[:, fc:fc + 1], rstd, ALU.mult, ALU.mult)
                    S.act(H[:, fc, sl], t, AF.Identity, bias=bcol[:, fc:fc + 1])

        def post_norm_residual(src, gcol, scr, hook=None):
            rstd = sb(scr, [128, TH])
            tmpr = sb(scr + 512, [128, TH])
            tm = [sb(scr + 1024 + i * 512, [128, TH]) for i in range(2)]
            for th in range(2):
                sl = slice(th * TH, (th + 1) * TH)
                rstd_from_bank(S.bank(6 + th), rstd, D, tmpr)
                for fc in range(16):
                    t = tm[fc % 2]
                    S.stt(t, src[:, fc, sl], gcol[:, fc:fc + 1], rstd, ALU.mult, ALU.mult)
                    S.tt(X[:, fc, sl], X[:, fc, sl], t, ALU.add)
                    if hook is not None:
                        hook()

        tbank = Ring([0, 1, 2, 3])
        nb = lambda: S.bank(tbank.next())

        try:
          for l in range(nl):
              S.dma("sp", PV, pv[l])
              WIN = wview(w_in, l)
              if l == 0:
                  WMOD = wview(w_mod, l)
                  abank = S.bank(0)
                  for tl in range(48):
                      wt = wbuf([128, 16, 256])
                      S.dma("pool", wt, WMOD[:, :, tl * 256:(tl + 1) * 256])
                      for jj in range(2):
                          j = tl * 2 + jj
                          for kc in range(16):
                              S.mm(abank[:, j:j + 1], wt[:, kc, jj * 128:(jj + 1) * 128], SILB[:, kc:kc + 1],
                                   start=(kc == 0), stop=(kc == 15), sig=(kc == 15 and jj == 1))
                  S.tt(ADA, abank[:, 0:96], PV[:, PV_BMOD:PV_BMOD + 96], ALU.add)
              else:
                  S.tt(ADA, ADA_NEXT, PV[:, PV_BMOD:PV_BMOD + 96], ALU.add)
              SH1, SC1, GA1, SH2, SC2, GA2 = [ADA[:, i * 16:(i + 1) * 16] for i in range(6)]
              gn = lambda j: PV[:, PV_GN + 16 * j:PV_GN + 16 * (j + 1)]
              dcol = lambda o, n=16: DER[:, o:o + n]
              S.stt(dcol(DR_A1), SC1, 1.0, gn(0), ALU.add, ALU.mult)
              S.tt(dcol(DR_GA1), GA1, gn(1), ALU.mult)
              S.stt(dcol(DR_A2), SC2, 1.0, gn(2), ALU.add, ALU.mult)
              S.tt(dcol(DR_GA2), GA2, gn(3), ALU.mult)
              S.act(dcol(DR_TMP), PV[:, PV_LAM:PV_LAM + 16], AF.Exp, scale=-1.0)
              S.act(dcol(DR_TMP), dcol(DR_TMP), AF.Ln, bias=ONEC)
              S.ts(dcol(DR_KL), dcol(DR_TMP), -8.0, None, ALU.mult)
              S.ts(dcol(DR_K2), dcol(DR_TMP), -16.0, None, ALU.mult)
              S.act(dcol(DR_NLG), PV[:, PV_DEC:PV_DEC + 16], AF.Exp, scale=-1.0)
              S.act(dcol(DR_NLG), dcol(DR_NLG), AF.Ln, bias=ONEC)
              S.ts(dcol(DR_LG), dcol(DR_NLG), -1.0, None, ALU.mult)
              if l == 0:
                  dbg("ada0", ADA)
              if l == 1:
                  dbg("ada1", ADA)
                  dbg("der0", DER)
              chk("ada")

              pre_norm(dcol(DR_A1), SH1, SCR_OFF)
              if l == 0:
                  dbg("h0", H)
              chk("prenorm")

              RS = YBR_OFF
              E2 = [sb(RS + i * 960, [128, 1920], BF16) for i in range(2)]
              RQ = sb(RS + 1920, [128, T], BF16)
              RK = sb(RS + 2432, [128, T], BF16)
              RV = sb(RS + 2944, [128, 8, 128], BF16)
              SRG = sb(RS + 3456, [128, T], BF16)
              S0 = sb(RS + 3968, [128, 16, 128], BF16)
              KD = [sb(RS + 4992 + i * 512, [128, 8, 128], BF16) for i in range(2)]
              STG = sb(RS + 6016, [128, 4, 2, 128])
              TB = [sb(SCR_OFF + i * 512, [128, TH]) for i in range(6)]
              TBB = [sb(SCR_OFF + 3072 + i * 256, [128, TH], BF16) for i in range(7)]
              BCOL = sb(SCR_OFF + 4864, [128, 16])
              WDEC = sb(SCR_OFF + 4880, [128, 4])
              DFB = sb(SCR_OFF + 4884, [128, 4])
              XBAND = sb(SCR_OFF + 4896, [128, 128])
              rslots = [WB_OFF, WB_OFF + 2048, SCR_OFF + 5120]
              rwi = [0]

              def rwbuf():
                  off = rslots[rwi[0] % 3]
                  rwi[0] += 1
                  return sb(off, [128, 16, 256], BF16)

              ada_bufs = [sb(SCR_OFF + 7168, [128, 8, 256], BF16), sb(RS + 7040, [128, 8, 256], BF16)]
              ada_i = [0]
              do_ada = (l + 1 < nl)
              abank6 = S.bank(6)
              if do_ada:
                  WMODN = wview(w_mod, l + 1)
                  S.dma("pool", ada_bufs[0], WMODN[:, 0:8, 0:256])

              def ada_tile_ap(i):
                  j2, kh = i // 2, i % 2
                  return WMODN[:, kh * 8:(kh + 1) * 8, j2 * 256:(j2 + 1) * 256]

              ada_calls = [0]

              def ada_step(n):
                  if not do_ada:
                      return
                  if n == 1:
                      ada_calls[0] += 1
                      if ada_calls[0] % 3 != 1:
                          return
                  for _ in range(n):
                      i = ada_i[0]
                      if i >= 32:
                          return
                      if i + 1 < 32:
                          S.dma("pool", ada_bufs[(i + 1) % 2], ada_tile_ap(i + 1))
                      bf = ada_bufs[i % 2]
                      j2, kh = i // 2, i % 2
                      for jj in range(2):
                          col = S.bank(6 + jj)[:, j2:j2 + 1]
                          for kc in range(8):
                              S.mm(col, bf[:, kc, jj * 128:(jj + 1) * 128], SILB[:, kh * 8 + kc:kh * 8 + kc + 1],
                                   start=(kh == 0 and kc == 0), stop=(kh == 1 and kc == 7), sig=(kc == 7 and jj == 1))
                      ada_i[0] += 1

              def ret_wdma(hd_):
                  wqk_ = rwbuf()
                  S.dma("pool", wqk_[:, :, 0:128], WIN[:, :, C_RQ + hd_ * 128:C_RQ + (hd_ + 1) * 128])
                  S.dma("pool", wqk_[:, :, 128:256], WIN[:, :, C_RK + hd_ * 128:C_RK + (hd_ + 1) * 128])
                  return wqk_

              def ret_wdma2(hd_):
                  wvg_ = rwbuf()
                  S.dma("pool", wvg_[:, :, 0:128], WIN[:, :, C_RV + hd_ * 128:C_RV + (hd_ + 1) * 128])
                  S.dma("pool", wvg_[:, :, 128:256], WIN[:, :, C_RG + hd_ * 128:C_RG + (hd_ + 1) * 128])
                  return wvg_

              nxt_qk = ret_wdma(0)
              nxt_vg = ret_wdma2(0)
              S.dma("pool", S0, sret[l].rearrange("d h k v -> k (d h) v"))
              S.ts(XBAND, XIB[:, 0:128], PIDX, None, ALU.subtract)
              rbank = Ring([0, 1, 2, 3])
              rb = lambda: S.bank(rbank.next())
              for hd in range(8):
                  wqk, wvg = nxt_qk, nxt_vg
                  if hd + 1 < 8:
                      nxt_qk = ret_wdma(hd + 1)
                  chk("ret_a1")
                  LGF = dcol(DR_LG + hd, 1)
                  LGB = dcol(DR_LG + 8 + hd, 1)
                  NLGB = dcol(DR_NLG + 8 + hd, 1)
                  for th in range(2):
                      sl = slice(th * TH, (th + 1) * TH)
                      pbk = []
                      for which in range(2):
                          bk = rb()
                          for kc in range(16):
                              S.mm(bk, wqk[:, kc, which * 128:(which + 1) * 128], H[:, kc, sl], start=(kc == 0), stop=(kc == 15))
                          xb = TBB[0] if which == 0 else TBB[6]
                          S.copy(xb, bk, eng="act")
                          pbk.append((bk, xb))
                      for which, dst in ((0, RQ), (1, RK)):
                          bk, xb = pbk[which]
                          b2 = rb()
                          S.mm(b2, RRET, xb, start=True, stop=True)
                          t0_, t1_ = (TB[0], TB[1]) if which == 0 else (TB[3], TB[4])
                          S.tt(t0_, bk, CSR[:, 0, sl], ALU.mult)
                          S.tt(t1_, b2, CSR[:, 1, sl], ALU.mult)
                          S.tt(dst[:, sl], t0_, t1_, ALU.add)
                      if th == 1 and hd + 1 < 8:
                          nxt_vg = ret_wdma2(hd + 1)
                      ada_step(1)
                  S.ts(BCOL[:, 0:3], GL[:, GL_PSH:GL_PSH + 3], NLGB, LNS, ALU.mult, ALU.add)
                  S.ts(BCOL[:, 3:6], GL[:, GL_PSH + 3:GL_PSH + 6], LGF, LNS, ALU.mult, ALU.add)
                  S.ts(BCOL[:, 6:12], BCOL[:, 0:6], LM, None, ALU.add)
                  S.ts(TB[2][:, 0:128], XBAND, LGF, 0.0, ALU.mult, ALU.min)
                  S.ts(TB[2][:, 128:256], XBAND, NLGB, 0.0, ALU.mult, ALU.min)
                  S.tt(TB[2][:, 0:128], TB[2][:, 0:128], TB[2][:, 128:256], ALU.add)
                  for th in range(2):
                      S.ts(DFB[:, 2 * th:2 * th + 1], LGF, float(th * 512 + 1), None, ALU.mult)
                      S.ts(DFB[:, 2 * th + 1:2 * th + 2], NLGB, float(th * 512 - 1024), None, ALU.mult)
                  S.act(TB[2][:, 256:384], TB[2][:, 0:128], AF.Exp, bias=LNSC)
                  for par in range(2):
                      E = E2[par]
                      S.stt(E[:, 896:1024], IDENT, SK, TB[2][:, 256:384], ALU.mult, ALU.add)
                      inseg = (1024, 1280) if par == 0 else (896, 1152)
                      for bi, (c0, c1) in enumerate(EBLK):
                          masked = not (c0 >= inseg[0] and c1 <= inseg[1])
                          sc = NLGB if bi < 3 else LGF
                          bcol_i = bi + (6 if masked else 0)
                          S.act(E[:, c0 - 128:c1 - 128], XIB[:, 0:c1 - c0], AF.Exp, bias=BCOL[:, bcol_i:bcol_i + 1], scale=sc)
                  S.act(WDEC[:, 0:2], GL[:, GL_POSF:GL_POSF + 2], AF.Exp, bias=LNSC, scale=LGF)
                  S.act(WDEC[:, 2:4], GL[:, GL_POSB:GL_POSB + 2], AF.Exp, bias=LNSC, scale=LGB)
                  qd = {}
                  for th in range(2):
                      sl = slice(th * TH, (th + 1) * TH)
                      S.act(TB[3], XIB, AF.Exp, bias=DFB[:, 2 * th:2 * th + 1], scale=LGF)
                      S.act(TB[4], XIB, AF.Exp, bias=DFB[:, 2 * th + 1:2 * th + 2], scale=NLGB)
                      qdf, qdb = (TBB[1], TBB[2]) if th == 0 else (TBB[0], TBB[6])
                      S.tt(qdf, RQ[:, sl], TB[3], ALU.mult)
                      S.tt(qdb, RQ[:, sl], TB[4], ALU.mult)
                      qd[th] = (qdf, qdb)
                  for th in range(2):
                      sl = slice(th * TH, (th + 1) * TH)
                      bk = rb()
                      for kc in range(16):
                          S.mm(bk, wvg[:, kc, 128:256], H[:, kc, sl], start=(kc == 0), stop=(kc == 15))
                      S.act(SRG[:, sl], bk, AF.Silu)
                      ada_step(1)
                  chk("ret_a3")
                  for g in range(2):
                      bk = rb()
                      for j in range(4):
                          tt_ = g * 4 + j
                          for kc in range(16):
                              S.mm(bk[:, j * 128:(j + 1) * 128], H[:, kc, tt_ * 128:(tt_ + 1) * 128], wvg[:, kc, 0:128],
                                   start=(kc == 0), stop=(kc == 15), sig=(kc == 15 and j == 3))
                      S.copy(RV[:, g * 4:(g + 1) * 4, :], bk.rearrange("p (a b) -> p a b", a=4), eng="act")
                      ada_step(1)
                  chk("ret_a")
                  kbank = S.bank(rbank.next(), dt=BF16)
                  for tt_ in range(8):
                      S.tr(kbank[:, tt_ * 128:(tt_ + 1) * 128], RK[:, tt_ * 128:(tt_ + 1) * 128], IDENT, sig=(tt_ == 7))
                  kb3 = kbank.rearrange("p (a b c) -> p a b c", a=4, b=2)
                  for dr in range(2):
                      kd3 = KD[dr].rearrange("p (a b) c -> p a b c", b=2)
                      for par in range(2):
                          S.ts(kd3[:, :, par, :], kb3[:, :, par, :], WDEC[:, dr * 2 + par:dr * 2 + par + 1], None, ALU.mult)
                  stb = [rb(), rb()]
                  for seg in range(4):
                      for dr in range(2):
                          idx = seg * 2 + dr
                          ob = stb[idx // 4][:, (idx % 4) * 128:(idx % 4 + 1) * 128]
                          for c2 in range(2):
                              S.mm(ob, KD[dr][:, seg * 2 + c2, :], RV[:, seg * 2 + c2, :], start=(c2 == 0), stop=(c2 == 1),
                                   sig=(c2 == 1 and idx % 4 == 3))
                  stg2 = STG.rearrange("p a b c -> p (a b) c")
                  S.copy(stg2[:, 0:4, :], stb[0].rearrange("p (a b) -> p a b", a=4), eng="act")
                  S.copy(stg2[:, 4:8, :], stb[1].rearrange("p (a b) -> p a b", a=4), eng="act")
                  S.dma("sp", reto[l][:, :, hd].rearrange("s d k v -> k s d v"), STG)
                  chk("ret_b")
                  for th in range(2):
                      sl = slice(th * TH, (th + 1) * TH)
                      qdf, qdb = qd[th]
                      obank = S.bank(4 + th)
                      sbks = {}

                      def emit_scores(si_):
                          sbks[si_] = rb()
                          S.mm(sbks[si_], RK[:, si_ * 128:(si_ + 1) * 128], RQ[:, sl], start=True, stop=True)

                      emit_scores(0)
                      emit_scores(1)
                      for si in range(8):
                          if si + 2 < 8:
                              emit_scores(si + 2)
                          sbk = sbks[si]
                          pT = TBB[3 + (si % 3)]
                          off = th * 512 - 128 * si + 896
                          S.tt(pT, sbk, E2[si % 2][:, off:off + 512], ALU.mult)
                          S.mm(obank, RV[:, si, :], pT, start=(si == 0), stop=False, sig=False)
                          if si in (1, 4, 6):
                              ada_step(1)
                      S.mm(obank, S0[:, hd, :], qdf, start=False, stop=False, sig=False)
                      S.mm(obank, S0[:, 8 + hd, :], qdb, start=False, stop=True)
                  for th in range(2):
                      sl = slice(th * TH, (th + 1) * TH)
                      obank = S.bank(4 + th)
                      o32, osq = TB[0], TB[1]
                      S.copy(o32, obank, eng="act")
                      S.act(osq, obank, AF.Square)
                      mbank = rb()
                      vbank = rb()
                      S.mm(mbank, ONESF, o32, start=True, stop=True)
                      S.mm(vbank, ONESF, osq, start=True, stop=True)
                      S.act(TB[5], mbank, AF.Square)
                      S.tt(o32, o32, mbank, ALU.subtract)
                      S.tt(TB[5], vbank, TB[5], ALU.subtract)
                      S.ts(TB[5], TB[5], 0.0, None, ALU.max)
                      S.act(TB[5], TB[5], AF.Sqrt, bias=EPSC)
                      S.recip(TB[1], TB[5])
                      S.tt(o32, o32, TB[1], ALU.mult)
                      S.tt(YBR[:, 2, hd, sl], o32, SRG[:, sl], ALU.mult)
              if do_ada:
                  ada_step(96)
                  an3 = ADA_NEXT[:, 0:32].rearrange("p (a b) -> p a b", b=2)
                  S.copy(an3[:, :, 0], S.bank(6)[:, 0:16])
                  S.copy(an3[:, :, 1], S.bank(7)[:, 0:16])
              if l == 0:
                  dbg("oret0", YBR[:, 2])
              chk("ret")

              MS = YBR_OFF
              CQN = sb(MS, [128, 4, T], BF16)
              CKVA = sb(MS + 2048, [128, 2, 1280], BF16)
              KRX = sb(MS + 3328, [128, 1280], BF16)
              M0 = SCR_OFF
              CQ32 = sb(M0, [128, 4, T])
              CKV32 = sb(M0 + 4096, [128, 2, T])
              KR32 = sb(M0, [64, T])
              NRS = sb(M0 + 6144, [128, TH])
              NTM = sb(M0 + 6656, [128, TH])
              NSQ = sb(M0 + 7168, [128, TH], BF16)
              KNT = sb(M0 + 1024, [128, 1280], BF16)
              VH = sb(M0 + 1664, [128, 10, 128], BF16)
              QNT = sb(M0 + 2304, [128, T], BF16)
              QRX = sb(M0 + 2816, [128, T], BF16)
              WQ = sb(M0 + 3328, [128, 4, 768], BF16)
              MT = [sb(M0 + 4864 + i * 512, [128, TH]) for i in range(3)]
              MTB = [sb(M0 + 6400 + i * 256, [128, TH], BF16) for i in range(4)]
              S.dma("pool", CKVA[:, :, 0:256], cckv[l].rearrange("(rc p) s -> p rc s", p=128))
              S.dma("pool", sb(MS + 3328, [64, 1280], BF16)[:, 0:256], ckr[l])
              S.dma("pool", sb(MS + 3328, [4, 1280], BF16, p0=64), mk)

              def small_norm(nch, c0, dst32, nfeat, emit):
                  for tl in range((nch + 1) // 2):
                      wt_ = wbuf([128, 16, 256])
                      S.dma("pool", wt_, WIN[:, :, c0 + tl * 256:c0 + (tl + 1) * 256])
                      for jj in range(2):
                          j = tl * 2 + jj
                          for th in range(2):
                              sl = slice(th * TH, (th + 1) * TH)
                              bk = nb()
                              for kc in range(16):
                                  S.mm(bk, wt_[:, kc, jj * 128:(jj + 1) * 128], H[:, kc, sl], start=(kc == 0), stop=(kc == 15))
                              S.copy(dst32[:, j, sl], bk, eng="act")
                              S.act(NSQ, bk, AF.Square)
                              S.mm(S.bank(6 + th), ONES, NSQ, start=(j == 0), stop=(j == nch - 1))
                  for th in range(2):
                      sl = slice(th * TH, (th + 1) * TH)
                      rstd_from_bank(S.bank(6 + th), NRS, nfeat, NTM)
                      for j in range(nch):
                          emit(j, th, sl, NRS)

              def emit_q(j, th, sl, rstd):
                  S.stt(CQN[:, j, sl], CQ32[:, j, sl], PV[:, PV_GQ + j:PV_GQ + j + 1], rstd, ALU.mult, ALU.mult)

              small_norm(4, C_CQ, CQ32, 512, emit_q)

              def emit_kv(j, th, sl, rstd):
                  S.stt(CKV32[:, j, sl], CKV32[:, j, sl], PV[:, PV_GKV + j:PV_GKV + j + 1], rstd, ALU.mult, ALU.mult)
                  S.copy(CKVA[:, j, 256 + th * TH:256 + (th + 1) * TH], CKV32[:, j, sl])

              small_norm(2, C_CKV, CKV32, 256, emit_kv)
              S.dma("sp", ckvo[l].rearrange("(rc p) t -> p rc t", p=128), CKV32)
              wt = wbuf([128, 16, 64])
              S.dma("pool", wt, WIN[:, :, C_KR:C_KR + 64])
              for th in range(2):
                  sl = slice(th * TH, (th + 1) * TH)
                  bk = nb()
                  for kc in range(16):
                      S.mm(bk[0:64, :], wt[:, kc, :], H[:, kc, sl], start=(kc == 0), stop=(kc == 15))
                  S.copy(KR32[:, sl], bk[0:64, :], eng="act")
                  xb = MTB[0]
                  S.copy(xb[0:64, :], bk[0:64, :], eng="act")
                  b2 = nb()
                  S.mm(b2[0:64, :], RMLA[0:64, 0:64], xb[0:64, :], start=True, stop=True)
                  S.tt(MT[0][0:64, :], bk[0:64, :], CSM[0:64, 0, sl], ALU.mult)
                  S.tt(MT[1][0:64, :], b2[0:64, :], CSM[0:64, 1, sl], ALU.mult)
                  S.tt(KRX[0:64, 256 + th * TH:256 + (th + 1) * TH], MT[0][0:64, :], MT[1][0:64, :], ALU.add)
              S.dma("sp", kro[l], KR32)
              wkv = wbuf([128, 2, 2048])
              S.dma("pool", wkv, w_ukv[l].rearrange("(rc p) n -> p rc n", p=128))
              S.dma("pool", sb(M0 + 2816, [4, T], BF16, p0=64), mq)
              for hd in range(8):
                  if hd % 4 == 0:
                      S.dma("pool", WQ, w_uq[l].rearrange("(kc p) n -> p kc n", p=128)[:, :, (hd // 4) * 768:(hd // 4 + 1) * 768])
                  hq = (hd % 4) * 192
                  for (s0, s1) in ((0, 512), (512, 1024), (1024, 1280)):
                      bk = nb()
                      for rc in range(2):
                          S.mm(bk[:, 0:s1 - s0], wkv[:, rc, hd * 256:hd * 256 + 128], CKVA[:, rc, s0:s1], start=(rc == 0), stop=(rc == 1))
                      S.copy(KNT[:, s0:s1], bk[:, 0:s1 - s0], eng="act")
                  for (a0, a1) in ((0, 4), (4, 8), (8, 10)):
                      bk = nb()
                      for sc_ in range(a0, a1):
                          for rc in range(2):
                              S.mm(bk[:, (sc_ - a0) * 128:(sc_ - a0 + 1) * 128], CKVA[:, rc, sc_ * 128:(sc_ + 1) * 128],
                                   wkv[:, rc, hd * 256 + 128:hd * 256 + 256], start=(rc == 0), stop=(rc == 1),
                                   sig=(rc == 1 and sc_ == a1 - 1))
                      n = a1 - a0
                      S.copy(VH[:, a0:a1, :], bk[:, 0:n * 128].rearrange("p (a b) -> p a b", a=n), eng="act")
                  for th in range(2):
                      sl = slice(th * TH, (th + 1) * TH)
                      bk = nb()
                      for kc in range(4):
                          S.mm(bk, WQ[:, kc, hq:hq + 128], CQN[:, kc, sl], start=(kc == 0), stop=(kc == 3))
                      S.copy(QNT[:, sl], bk, eng="act")
                      bk = nb()
                      for kc in range(4):
                          S.mm(bk[0:64, :], WQ[:, kc, hq + 128:hq + 192], CQN[:, kc, sl], start=(kc == 0), stop=(kc == 3))
                      xb = MTB[0]
                      S.copy(xb[0:64, :], bk[0:64, :], eng="act")
                      b2 = nb()
                      S.mm(b2[0:64, :], RMLA[0:64, 0:64], xb[0:64, :], start=True, stop=True)
                      S.tt(MT[0][0:64, :], bk[0:64, :], CSM[0:64, 0, sl], ALU.mult)
                      S.tt(MT[1][0:64, :], b2[0:64, :], CSM[0:64, 1, sl], ALU.mult)
                      S.tt(QRX[0:64, sl], MT[0][0:64, :], MT[1][0:64, :], ALU.add)
                  for th in range(2):
                      sl = slice(th * TH, (th + 1) * TH)
                      obank = S.bank(4 + 2 * th)
                      dbank = S.bank(5 + 2 * th)
                      sbks = {}

                      def emit_sc(s_):
                          sbks[s_] = nb()
                          S.mm(sbks[s_], KNT[:, s_ * 128:(s_ + 1) * 128], QNT[:, sl], start=True, stop=False, sig=False)
                          S.mm(sbks[s_], KRX[0:68, s_ * 128:(s_ + 1) * 128], QRX[0:68, sl], start=False, stop=True)

                      emit_sc(0)
                      emit_sc(1)
                      for sc_ in range(10):
                          if sc_ + 2 < 10:
                              emit_sc(sc_ + 2)
                          sbk = sbks[sc_]
                          pT = MTB[1 + (sc_ % 3)]
                          S.act(pT, sbk, AF.Exp, scale=MLA_SCALE)
                          S.mm(obank, VH[:, sc_, :], pT, start=(sc_ == 0), stop=(sc_ == 9))
                          S.mm(dbank, ONES, pT, start=(sc_ == 0), stop=(sc_ == 9))
                      S.recip(MT[2], dbank)
                      S.tt(YBR[:, 1, hd, sl], obank, MT[2], ALU.mult)
              if l == 0:
                  dbg("omla0", YBR[:, 1])
              chk("mla")

              L0 = SCR_OFF
              XR = sb(L0, [128, 4, 262], BF16)
              DG = sb(L0 + 524, [128, 4, 128], BF16)
              GG = sb(L0 + 1048, [128, T], BF16)
              XC = sb(L0 + 1560, [128, 4, 256])
              XCB = sb(L0 + 2584, [128, T], BF16)
              RR = sb(L0 + 3096, [128, T])
              IG = sb(L0 + 4120, [128, T])
              A2 = sb(L0 + 5144, [128, T])
              HSF = sb(L0 + 6168, [128, T])
              HSB = sb(L0 + 7192, [128, T])
              assert 8216 <= SCR_W
              wax_slot = wslot[0] % 2
              wslot[0] += 1
              wax = wbuf([128, 32, 128], slot=wax_slot)
              S.dma("pool", wax[:, 0:16, :], lru_wa[l].rearrange("d n c e -> c (d n) e"))
              S.dma("pool", wax[:, 16:32, :], lru_wx[l].rearrange("d n c e -> c (d n) e"))
              S.memset(XR[:, 0, 0:3], 0.0)
              S.memset(XR[:, 3, 259:262], 0.0)
              xcf = XC.rearrange("p a b -> p (a b)")
              lbank = Ring(range(8))
              nb_saved = nb
              nb = lambda: S.bank(lbank.next())
              for c in range(8):
                  wt = wbuf([128, 16, 256], slot=1 - wax_slot)
                  S.dma("pool", wt[:, :, 0:128], WIN[:, :, c * 128:(c + 1) * 128])
                  S.dma("pool", wt[:, :, 128:256], WIN[:, :, 1024 + c * 128:1024 + (c + 1) * 128])
                  for th in range(2):
                      sl = slice(th * TH, (th + 1) * TH)
                      bk = nb()
                      for kc in range(16):
                          S.mm(bk, wt[:, kc, 0:128], H[:, kc, sl], start=(kc == 0), stop=(kc == 15))
                      S.copy(XR[:, 2 * th:2 * th + 2, 3:259], bk.rearrange("p (a b) -> p a b", a=2), eng="act")
                      bk = nb()
                      for kc in range(16):
                          S.mm(bk, wt[:, kc, 128:256], H[:, kc, sl], start=(kc == 0), stop=(kc == 15))
                      S.act(GG[:, sl], bk, AF.Gelu)
                  S.ts(XR[:, 1:4, 0:3], XR[:, 0:3, 256:259], CROSS, None, ALU.mult)
                  S.ts(XR[:, 0:3, 259:262], XR[:, 1:4, 3:6], CROSS, None, ALU.mult)
                  for dr in range(2):
                      cw = lambda j: PV[:, PV_CW + (dr * 4 + j) * 8 + c:PV_CW + (dr * 4 + j) * 8 + c + 1]
                      pcol = lambda base: PV[:, base + dr * 8 + c:base + dr * 8 + c + 1]
                      for j in range(4):
                          S.ts(DG[:, j, :], IDENT, cw(j), None, ALU.mult)
                      for th in range(2):
                          sl = slice(th * TH, (th + 1) * TH)
                          bk = nb()
                          for j in range(4):
                              o_ = j if dr == 0 else 6 - j
                              S.mm(bk, DG[:, j, :], XR[:, 2 * th:2 * th + 2, o_:o_ + 256], start=(j == 0), stop=(j == 3))
                          S.act(XCB[:, sl], bk, AF.Identity, bias=pcol(PV_CB))
                          S.ts(xcf[:, sl], bk, pcol(PV_CB), None, ALU.add)
                      for th in range(2):
                          sl = slice(th * TH, (th + 1) * TH)
                          bk = nb()
                          S.mm(bk, wax[:, dr * 8 + c, :], XCB[:, sl], start=True, stop=True)
                          S.act(RR[:, sl], bk, AF.Sigmoid, bias=pcol(PV_BA))
                          bk = nb()
                          S.mm(bk, wax[:, 16 + dr * 8 + c, :], XCB[:, sl], start=True, stop=True)
                          S.act(IG[:, sl], bk, AF.Sigmoid, bias=pcol(PV_BX))
                      kl = dcol(DR_KL + dr * 8 + c, 1)
                      k2 = dcol(DR_K2 + dr * 8 + c, 1)
                      S.act(A2, RR, AF.Exp, scale=k2)
                      S.act(RR, RR, AF.Exp, scale=kl)
                      S.ts(A2, A2, 1.0, None, ALU.min)
                      S.act(A2, A2, AF.Sqrt, bias=ONEC, scale=-1.0)
                      S.tt(IG, IG, xcf, ALU.mult)
                      S.tt(IG, IG, A2, ALU.mult)
                      a3 = RR.rearrange("p (a b) -> p a b", a=4)
                      h0 = PV[:, PV_H0 + dr * 8 + c:PV_H0 + dr * 8 + c + 1]
                      lo = LRUO.rearrange("p (d c s) -> p d c s", d=2, c=8)[:, dr, c, :]
                      if dr == 0:
                          S.ts(a3[:, 1:4, 0:1], a3[:, 1:4, 0:1], CROSS, None, ALU.mult)
                          S.scan(HSF, RR, IG, h0)
                          S.copy(lo, HSF.rearrange("p (a b) -> p a b", a=4)[:, :, 255])
                      else:
                          S.ts(a3[:, 0:3, 255:256], a3[:, 0:3, 255:256], CROSS, None, ALU.mult)
                          S.scan(HSB[:, ::-1], RR[:, ::-1], IG[:, ::-1], h0)
                          S.copy(lo, HSB.rearrange("p (a b) -> p a b", a=4)[:, :, 0])
                  S.tt(HSF, HSF, HSB, ALU.add)
                  S.tt(YBR[:, 0, c, :], HSF, GG, ALU.mult)
              S.dma("sp", lruo[:, l * 64:(l + 1) * 64], LRUO)
              nb = nb_saved
              if l == 0:
                  dbg("ylru0", YBR[:, 0])
              chk("lru")

              MG = sb(SCR_OFF, [128, 16, T], BF16)
              mbank = Ring([0, 1, 2, 3, 6, 7])
              nb = lambda: S.bank(mbank.next())
              assert SCR_W >= 8192
              SG = sb(WB_OFF + 1536, [128, TH])
              MA = sb(WB_OFF + 2048 + 1536, [128, TH])
              for fc in range(16):
                  for b in range(3):
                      slot = wslot[0] % 2
                      wslot[0] += 1
                      wg = sb(WB_OFF + slot * 2048, [128, 16, 128], BF16)
                      wb_ = sb(WB_OFF + slot * 2048 + 1024, [128, 8, 128], BF16)
                      c0 = C_G + b * 2048 + fc * 128
                      S.dma("pool", wg, WIN[:, :, c0:c0 + 128])
                      S.dma("pool", wb_, w_br[b][l].rearrange("(kc p) n -> p kc n", p=128)[:, :, fc * 128:(fc + 1) * 128])
                      for th in range(2):
                          sl = slice(th * TH, (th + 1) * TH)
                          gb = nb()
                          for kc in range(16):
                              S.mm(gb, wg[:, kc, :], H[:, kc, sl], start=(kc == 0), stop=(kc == 15))
                          ub = nb()
                          for kc in range(8):
                              S.mm(ub, wb_[:, kc, :], YBR[:, b, kc, sl], start=(kc == 0), stop=(kc == 7))
                          S.act(SG, gb, AF.Sigmoid)
                          acc = S.bank(4 + th)
                          if b == 0:
                              S.tt(acc, ub, SG, ALU.mult)
                          elif b == 1:
                              S.tt(MA, ub, SG, ALU.mult)
                              S.tt(acc, acc, MA, ALU.add)
                          else:
                              S.tt(MA, ub, SG, ALU.mult)
                              S.tt(MG[:, fc, sl], acc, MA, ALU.add)
              nb = nb_saved
              if l == 0:
                  dbg("mg0", MG)
              chk("merge")
              WO = wview(w_out, l)
              do_ada2 = (l + 1 < nl)
              ada2_i = [0]
              if do_ada2:
                  WMODN2 = wview(w_mod, l + 1)
                  ada2_bufs = [sb(YBR_OFF + 8192 + i * 2048, [128, 16, 256], BF16) for i in range(2)]
                  abank4 = S.bank(4)
                  S.dma("pool", ada2_bufs[0], WMODN2[:, :, 16 * 256:17 * 256])

              def ada2_step(n=1):
                  if not do_ada2:
                      return
                  for _ in range(n):
                      i = ada2_i[0]
                      if i >= 32:
                          return
                      if i + 1 < 32:
                          S.dma("pool", ada2_bufs[(i + 1) % 2], WMODN2[:, :, (17 + i) * 256:(18 + i) * 256])
                      bf = ada2_bufs[i % 2]
                      for jj in range(2):
                          j = 2 * i + jj
                          for kc in range(16):
                              S.mm(abank4[:, j:j + 1], bf[:, kc, jj * 128:(jj + 1) * 128], SILB[:, kc:kc + 1],
                                   start=(kc == 0), stop=(kc == 15), sig=(kc == 15 and jj == 1))
                      ada2_i[0] += 1
              sqb = [sb(WB_OFF + 1536 + i * 2048, [128, TH], BF16) for i in range(2)]
              obank_ring = Ring([0, 1, 2, 3, 5])
              nb = lambda: S.bank(obank_ring.next())
              for fc2 in range(16):
                  slot = wslot[0] % 2
                  wslot[0] += 1
                  wt = sb(WB_OFF + slot * 2048, [128, 16, 128], BF16)
                  S.dma("pool", wt, WO[:, :, fc2 * 128:(fc2 + 1) * 128])
                  for th in range(2):
                      sl = slice(th * TH, (th + 1) * TH)
                      bk = nb()
                      for kc in range(16):
                          S.mm(bk, wt[:, kc, :], MG[:, kc, sl], start=(kc == 0), stop=(kc == 15))
                      S.copy(U[:, fc2, sl], bk, eng="act")
                      s = sqb[th]
                      S.act(s, bk, AF.Square)
                      S.mm(S.bank(6 + th), ONES, s, start=(fc2 == 0), stop=(fc2 == 15))
                  ada2_step(1)
              if l == 0:
                  dbg("u0", U)
              nb = nb_saved
              post_norm_residual(U, dcol(DR_GA1), SCR_OFF, hook=ada2_step)
              if l == 0:
                  dbg("x0a", X)
              chk("wout")

              FB = YBR_OFF
              YACC = sb(FB, [128, 16, T])
              AG = [sb(FB + 16384 + i * 2048, [128, 4, T], BF16) for i in range(2)]
              FW = FB + 16384 + 4096
              assert FW + 4096 <= ARENA
              pre_norm(dcol(DR_A2), SH2, FB)
              if do_ada2:
                  ada2_step(32)
                  S.copy(ADA_NEXT[:, 32:96], abank4[:, 0:64])
              W1 = wview(w_ff1, l)
              fbank = Ring([0, 1, 2, 3, 4, 5])
              nb = lambda: S.bank(fbank.next())
              fslot = 0
              for g in range(16):
                  ag = AG[g % 2]
                  for t2 in range(2):
                      wt = sb(FW + (fslot % 2) * 2048, [128, 16, 256], BF16)
                      fslot += 1
                      j0 = g * 4 + t2 * 2
                      S.dma("pool", wt, W1[:, :, j0 * 128:(j0 + 2) * 128])
                      for jj in range(2):
                          for th in range(2):
                              sl = slice(th * TH, (th + 1) * TH)
                              bk = nb()
                              for kc in range(16):
                                  S.mm(bk, wt[:, kc, jj * 128:(jj + 1) * 128], H[:, kc, sl], start=(kc == 0), stop=(kc == 15))
                              dst = ag[:, t2 * 2 + jj, sl]
                              S.act(dst, bk, AF.Relu)
                              S.stt(dst, bk, 0.0, dst, ALU.max, ALU.mult)
                  for q4 in range(4):
                      wt = sb(FW + (fslot % 2) * 2048, [128, 4, 512], BF16)
                      fslot += 1
                      S.dma("pool", wt, w_ff2[l][g * 512:(g + 1) * 512, :].rearrange("(jc p) n -> p jc n", p=128)[:, :, q4 * 512:(q4 + 1) * 512])
                      for f4 in range(4):
                          fc2 = q4 * 4 + f4
                          for th in range(2):
                              sl = slice(th * TH, (th + 1) * TH)
                              bk = nb()
                              for jc in range(4):
                                  S.mm(bk, wt[:, jc, f4 * 128:(f4 + 1) * 128], ag[:, jc, sl], start=(jc == 0), stop=(jc == 3))
                              if g == 0:
                                  S.copy(YACC[:, fc2, sl], bk, eng="act")
                              else:
                                  S.tt(YACC[:, fc2, sl], YACC[:, fc2, sl], bk, ALU.add)
              zs = [sb(FB + 16384 + 2048 + i * 256, [128, TH], BF16) for i in range(2)]
              for th in range(2):
                  sl = slice(th * TH, (th + 1) * TH)
                  for fc in range(16):
                      s = zs[fc % 2]
                      S.act(s, YACC[:, fc, sl], AF.Square)
                      S.mm(S.bank(6 + th), ONES, s, start=(fc == 0), stop=(fc == 15))
              nb = nb_saved
              post_norm_residual(YACC, dcol(DR_GA2), FB + 16384)
              if l == 0:
                  dbg("x0b", X)
              chk("ffn")
        except (_Stop, _StopOps):
            S.stop_at = None

        S.dma("sp", yT.rearrange("(c p) t -> p c t", p=128), X)
        S.finish()
        stats = dict(issued=dict(S.nissued), waits=S.nwaits)
    return nc, stats


def _ppack(v, n):
    return np.ascontiguousarray(np.asarray(v, np.float32).reshape(n, 128).T)


def _rope_tables(rotate):
    t = np.arange(T)
    row = (t // 64).astype(np.float64)
    col = (t % 64).astype(np.float64)
    out = np.zeros((128, 4, T), np.float32)
    for base, dim in ((0, 128), (2, 64)):
        half = dim // 2
        inv = 10000.0 ** (-np.arange(0, half, 2, dtype=np.float64) / half)
        nf = half // 2
        for m in range(dim):
            pos = row if m < half else col
            i = (m % half) % nf
            ang = pos * inv[i] if rotate else np.zeros(T)
            out[m, base, :] = np.cos(ang)
            out[m, base + 1, :] = np.sin(ang)
        if dim < 128:
            out[dim:, base, :] = 1.0
    return out


def _rot_lhsT(dim):
    half = dim // 2
    nf = half // 2
    m = np.zeros((128, 128), np.float32)
    for o in (0, half):
        for i in range(nf):
            m[o + nf + i, o + i] = -1.0
            m[o + i, o + nf + i] = 1.0
    return m


def _core_inputs(inp, core, nl):
    prompt = core < 4
    p = np.arange(128, dtype=np.float32)
    if prompt:
        x = np.asarray(inp["x_prompt"][4 * core:4 * core + 4], np.float32).reshape(T, D)
        cond = np.asarray(inp["c_ctx"], np.float32)
    else:
        b = core - 4
        x = np.asarray(inp["x_sample"][b], np.float32)
        cond = np.asarray(inp["c"][b], np.float32)
    m = {}
    m["xT"] = np.ascontiguousarray(x.T)
    pvv = np.zeros((nl, 128, PVL), np.float32)
    for l in range(nl):
        for j in range(4):
            pvv[l, :, PV_GN + 16 * j:PV_GN + 16 * (j + 1)] = _ppack(inp["g_norm"][l, j], 16)
        pvv[l, :, PV_BMOD:PV_BMOD + 96] = _ppack(inp["b_mod"][l], 96)
        for d in range(2):
            for j in range(4):
                pvv[l, :, PV_CW + (d * 4 + j) * 8:PV_CW + (d * 4 + j) * 8 + 8] = _ppack(inp["lru_conv_w"][l, d, j], 8)
            pvv[l, :, PV_CB + d * 8:PV_CB + d * 8 + 8] = _ppack(inp["lru_conv_b"][l, d], 8)
            pvv[l, :, PV_BA + d * 8:PV_BA + d * 8 + 8] = _ppack(inp["lru_ba"][l, d], 8)
            pvv[l, :, PV_BX + d * 8:PV_BX + d * 8 + 8] = _ppack(inp["lru_bx"][l, d], 8)
            pvv[l, :, PV_LAM + d * 8:PV_LAM + d * 8 + 8] = _ppack(inp["lru_lam"][l, d], 8)
            if not prompt:
                pvv[l, :, PV_H0 + d * 8:PV_H0 + d * 8 + 8] = _ppack(inp["state_lru"][core - 4, l, d], 8)
            pvv[l, :, PV_DEC + d * 8:PV_DEC + d * 8 + 8] = np.asarray(inp["ret_decay"][l, d], np.float32)[None, :]
        pvv[l, :, PV_GQ:PV_GQ + 4] = _ppack(inp["mla_gq"][l], 4)
        pvv[l, :, PV_GKV:PV_GKV + 2] = _ppack(inp["mla_gkv"][l], 2)
    m["pv"] = pvv
    g = np.zeros((128, GLW), np.float32)
    g[:, GL_COND:GL_COND + 16] = _ppack(cond, 16)
    cross = 0.0 if prompt else 1.0
    g[:, GL_CROSS] = cross
    g[:, GL_LM] = (cross - 1.0) * (-LMV)
    g[:, GL_PIDX] = p
    for par in range(2):
        g[:, GL_POSF + par] = 255 - par * 128 - p
        g[:, GL_POSB + par] = par * 128 + p
    for bi, (c0, c1) in enumerate(EBLK):
        g[:, GL_PSH + bi] = c0 - 1024 - p
    m["gl"] = g
    cst = np.zeros((128, 5, 128), np.float32)
    cst[:, 0, :] = np.eye(128, dtype=np.float32)
    cst[:, 1, :] = 1.0
    cst[:, 2, :] = _rot_lhsT(128)
    cst[:, 3, :] = _rot_lhsT(64)
    cst[:, 4, :] = 1.0 / 128.0
    m["cst"] = cst
    m["cs"] = _rope_tables(rotate=not prompt)
    m["xib"] = np.ascontiguousarray(np.broadcast_to(np.arange(512, dtype=np.float32)[None, :], (128, 512)))
    mqv = np.zeros((4, T), np.float32)
    for gseg in range(4):
        mqv[gseg, gseg * 256:(gseg + 1) * 256] = 1.0
    m["mq"] = mqv
    mkv = np.zeros((4, 1280), np.float32)
    if prompt:
        mkv[:, :] = -BIG
        for gseg in range(4):
            mkv[gseg, 256 + gseg * 256:256 + (gseg + 1) * 256] = 0.0
    m["mk"] = mkv
    if prompt:
        m["cckv"] = np.zeros((nl, 256, 256), np.float32)
        m["ckr"] = np.zeros((nl, 64, 256), np.float32)
        m["sret"] = np.zeros((nl, 2, 8, 128, 128), np.float32)
    else:
        b = core - 4
        m["cckv"] = np.ascontiguousarray(np.asarray(inp["cache_mla_ckv"][b, :nl], np.float32).transpose(0, 2, 1))
        m["ckr"] = np.ascontiguousarray(np.asarray(inp["cache_mla_krope"][b, :nl], np.float32).transpose(0, 2, 1))
        m["sret"] = np.ascontiguousarray(np.asarray(inp["state_ret"][b, :nl], np.float32))
    return m


_WNAMES = ["w_mod", "w_in", "lru_wa", "lru_wx", "mla_wuq", "mla_wukv", "w_br_lru", "w_br_mla", "w_br_ret",
           "w_out", "w_ff1", "w_ff2"]


def run_cores(inp, cores, nl, dbg_names=(), trace=False, stop_after=None):
    nc, stats = build(nl, dbg_names, stop_after)
    shared = {n: np.ascontiguousarray(np.asarray(inp[n], np.float32)[:nl]) for n in _WNAMES}
    in_maps = []
    for c in cores:
        m = _core_inputs(inp, c, nl)
        m.update(shared)
        in_maps.append(m)
    res = run_bass_kernel_spmd(nc, in_maps, core_ids=list(range(len(cores))), trace=trace)
    return res, stats


def kernel(**inputs):
    nl = DEPTH
    res, _ = run_cores(inputs, list(range(8)), nl)
    r = res.results
    y_prompt = np.zeros((16, 256, D), np.float32)
    y_sample = np.zeros((4, T, D), np.float32)
    new_ckv = np.zeros((16, nl, 256, 256), np.float32)
    new_kr = np.zeros((16, nl, 256, 64), np.float32)
    new_lru = np.zeros((16, nl, 2, 1024), np.float32)
    new_ret = np.zeros((16, nl, 2, 8, 128, 128), np.float32)
    for c in range(8):
        o = r[c]
        y = np.asarray(o["yT"]).T
        if c < 4:
            y_prompt[4 * c:4 * c + 4] = y.reshape(4, 256, D)
            ck = np.asarray(o["ckvo"])
            new_ckv[4 * c:4 * c + 4] = ck.reshape(nl, 256, 4, 256).transpose(2, 0, 3, 1)
            kr = np.asarray(o["kro"])
            new_kr[4 * c:4 * c + 4] = kr.reshape(nl, 64, 4, 256).transpose(2, 0, 3, 1)
            lr = np.asarray(o["lruo"]).reshape(128, 4, 2, 8, 4)[:, :nl]
            new_lru[4 * c:4 * c + 4] = lr.transpose(4, 1, 2, 3, 0).reshape(4, nl, 2, 1024)
            rt = np.asarray(o["reto"])
            new_ret[4 * c:4 * c + 4] = rt.transpose(1, 0, 2, 3, 4, 5)
        else:
            y_sample[c - 4] = y
    return (y_prompt, y_sample, new_ckv, new_kr, new_lru, new_ret)
```

```python
import bisect
from contextlib import ExitStack
import numpy as np
import concourse.bass as bass
import concourse.mybir as mybir
from concourse.bass_utils import run_bass_kernel_spmd

F32 = mybir.dt.float32
BF16 = mybir.dt.bfloat16
AF = mybir.ActivationFunctionType
ALU = mybir.AluOpType
ESZ = {F32: 4, BF16: 2}

D = 2048
T = 1024
TH = 512
DEPTH = 4
N_IN = 13120
EPS = 1e-6
C_CQ, C_CKV, C_KR, C_RQ, C_RK, C_RV, C_RG, C_G = 2048, 2560, 2816, 2880, 3904, 4928, 5952, 6976
SK = 128.0 ** -0.5
LNS = float(np.log(SK))
MLA_SCALE = 192.0 ** -0.5
BIG = 8192.0
LMV = -30000.0

X_OFF = 0
PV_OFF = 16384
PVL = 336
GL_OFF = PV_OFF + PVL
GLW = 32
CSR_OFF = GL_OFF + GLW
CSM_OFF = CSR_OFF + 1024
CB_OFF = CSM_OFF + 1024
ONESF_OFF = CB_OFF + 256
XIB_OFF = ONESF_OFF + 128
ADA_OFF = XIB_OFF + 512
DER_OFF = ADA_OFF + 96
SILB_OFF = DER_OFF + 160
LRUO_OFF = SILB_OFF + 8
PERS_END = 20224
assert LRUO_OFF + 256 <= PERS_END
H_OFF = PERS_END
YBR_OFF = H_OFF + 8192
WB_OFF = YBR_OFF + 12288
SCR_OFF = WB_OFF + 4096
ARENA = 53200
SCR_W = ARENA - SCR_OFF

PV_GN = 0
PV_BMOD = 64
PV_CW = 160
PV_CB = 224
PV_BA = 240
PV_BX = 256
PV_LAM = 272
PV_GQ = 288
PV_GKV = 292
PV_H0 = 294
PV_DEC = 310
GL_COND = 0
GL_CROSS = 16
GL_LM = 17
GL_PIDX = 18
GL_POSF = 19
GL_POSB = 21
GL_PSH = 23
DR_A1, DR_GA1, DR_A2, DR_GA2, DR_KL, DR_K2, DR_LG, DR_NLG, DR_TMP = 0, 16, 32, 48, 64, 80, 96, 112, 128

EBLK = [(128, 640), (640, 896), (896, 1024), (1152, 1280), (1280, 1792), (1792, 2048)]


class _StopOps(Exception):
    pass


class Sched:
    ENGS = ("pe", "dve", "act", "pool", "sp")
    BLK = 1024

    def __init__(self, nc, es, arena_words, n_dma_sems=40):
        self.nc = nc
        self.eng = dict(pe=nc.tensor, dve=nc.vector, act=nc.scalar, pool=nc.gpsimd, sp=nc.sync)
        self.sem = {e: es.enter_context(nc.semaphore("s_" + e)) for e in self.ENGS}
        self.nsig = {e: 0 for e in self.ENGS}
        self.nissued = {e: 0 for e in self.ENGS}
        self.last_inst = {e: None for e in self.ENGS}
        self.last_signaled = {e: True for e in self.ENGS}
        self.sig_idx = {e: [] for e in self.ENGS}
        self.sig_val = {e: [] for e in self.ENGS}
        self.waited = {e: {} for e in self.ENGS}
        self.dsem = [es.enter_context(nc.semaphore("d%d" % i)) for i in range(n_dma_sems)]
        self.dcount = [0] * n_dma_sems
        self.dnext = 0
        self.lastw = {}
        self.rd_c = {}
        self.rd_d = {}
        self.nwaits = 0
        self.arena_t = es.enter_context(nc.sbuf_tensor("arena", [128, arena_words], F32))
        self.psum_t = es.enter_context(nc.psum_tensor("psum", [128, 8, 512], F32))

    def sb(self, word_off, shape, dt=F32, p0=0):
        n = 1
        for s in shape[1:]:
            n *= s
        words = (n * ESZ[dt] + 3) // 4
        assert word_off + words <= ARENA, (word_off, shape)
        v = self.arena_t[p0:p0 + shape[0], word_off:word_off + words]
        if dt != F32:
            v = v.bitcast(dt)
        if len(shape) > 2:
            names = " ".join("d%d" % i for i in range(len(shape) - 1))
            kw = {"d%d" % i: shape[i + 1] for i in range(len(shape) - 1)}
            v = v.rearrange("p (%s) -> p %s" % (names, names), **kw)
        return v

    def bank(self, b, parts=128, dt=F32):
        v = self.psum_t[0:parts, b, :]
        if dt != F32:
            v = v.bitcast(dt)
        return v

    def keys(self, ap):
        t = getattr(ap, "tensor", None)
        name = t.name if t is not None else None
        if name not in ("arena", "psum"):
            return ()
        es_ = ESZ[ap.dtype]
        pat = ap.ap
        pstride = pat[0][0]
        off = ap.offset % pstride
        lo = hi = off
        for st, cnt in pat[1:]:
            d = st * (cnt - 1)
            if d < 0:
                lo += d
            else:
                hi += d
        lo_b = lo * es_
        hi_b = (hi + 1) * es_ - 1
        if name == "psum":
            return [("P", b) for b in range(lo_b // 2048, hi_b // 2048 + 1)]
        return [("S", b) for b in range(lo_b // self.BLK, hi_b // self.BLK + 1)]

    def _resolve(self, tok):
        if tok[0] == "d":
            return ("d", tok[1]), self.dsem[tok[1]], tok[2]
        _, e, idx = tok
        i = bisect.bisect_left(self.sig_idx[e], idx)
        if i == len(self.sig_idx[e]):
            assert not self.last_signaled[e]
            self.last_inst[e].then_inc(self.sem[e], 1)
            self.nsig[e] += 1
            self.sig_idx[e].append(self.nissued[e] - 1)
            self.sig_val[e].append(self.nsig[e])
            self.last_signaled[e] = True
        return ("c", e), self.sem[e], self.sig_val[e][i]

    def _wait(self, eng, tok):
        if tok[0] == "c" and tok[1] == eng and eng == "pe":
            return
        key, sem, val = self._resolve(tok)
        if self.waited[eng].get(key, 0) >= val:
            return
        self.waited[eng][key] = val
        self.eng[eng].wait_ge(sem, val)
        self.nwaits += 1

    def _deps(self, eng, rkeys, wkeys):
        toks = set()
        for k in rkeys:
            t = self.lastw.get(k)
            if t is not None:
                toks.add(t)
            if k[0] == "P":
                for e2, t2 in self.rd_c.get(k, {}).items():
                    if e2 != eng:
                        toks.add(t2)
        for k in wkeys:
            t = self.lastw.get(k)
            if t is not None:
                toks.add(t)
            toks.update(self.rd_c.get(k, {}).values())
            toks.update(self.rd_d.get(k, ()))
        for t in sorted(toks, key=lambda t: (t[0], str(t[1]), t[2])):
            self._wait(eng, t)

    def _record(self, tok, rkeys, wkeys):
        for k in rkeys:
            if tok[0] == "c":
                self.rd_c.setdefault(k, {})[tok[1]] = tok
            else:
                self.rd_d.setdefault(k, []).append(tok)
        for k in wkeys:
            self.lastw[k] = tok
            self.rd_c[k] = {}
            self.rd_d[k] = []

    def _rw(self, outs, ins):
        rk, wk = [], []
        for a in ins:
            rk.extend(self.keys(a))
        for a in outs:
            wk.extend(self.keys(a))
        return rk, wk

    def op(self, eng, fn, outs=(), ins=(), sig=None):
        if getattr(self, 'stop_at', None) is not None:
            if self.opcount == self.stop_at:
                raise _StopOps()
            self.opcount += 1
        if sig is None:
            sig = (eng != "pe")
        rk, wk = self._rw(outs, ins)
        self._deps(eng, rk, wk)
        inst = fn()
        idx = self.nissued[eng]
        self.nissued[eng] += 1
        self.last_inst[eng] = inst
        if sig:
            inst.then_inc(self.sem[eng], 1)
            self.nsig[eng] += 1
            self.sig_idx[eng].append(idx)
            self.sig_val[eng].append(self.nsig[eng])
            self.last_signaled[eng] = True
        else:
            self.last_signaled[eng] = False
        self._record(("c", eng, idx), rk, wk)
        return inst

    def dma(self, q, out, in_, **kw):
        slot = self.dnext
        self.dnext = (self.dnext + 1) % len(self.dsem)
        if self.dcount[slot] > 0:
            self._wait(q, ("d", slot, 16 * self.dcount[slot]))
        rk, wk = self._rw([out], [in_])
        self._deps(q, rk, wk)
        self.dcount[slot] += 1
        tok = ("d", slot, 16 * self.dcount[slot])
        self.eng[q].dma_start(out=out, in_=in_, **kw).then_inc(self.dsem[slot], 16)
        self._record(tok, rk, wk)
        return tok

    def finish(self):
        for slot in range(len(self.dsem)):
            if self.dcount[slot] > 0:
                self._wait("sp", ("d", slot, 16 * self.dcount[slot]))
        for e in self.ENGS:
            if e != "sp" and self.nissued[e] > 0:
                self._wait("sp", ("c", e, self.nissued[e] - 1))

    def mm(self, out, lhsT, rhs, start, stop, sig=None):
        nc = self.nc
        if sig is None:
            sig = bool(stop)
        return self.op("pe", lambda: nc.tensor.matmul(out, lhsT, rhs, start=start, stop=stop),
                       outs=[out], ins=[lhsT, rhs], sig=sig)

    def tr(self, out, in_, ident, sig=False):
        nc = self.nc
        return self.op("pe", lambda: nc.tensor.transpose(out, in_, ident), outs=[out], ins=[in_, ident], sig=sig)

    def act(self, out, in_, func, bias=None, scale=None):
        nc = self.nc
        kw = {}
        ins = [in_]
        if bias is not None:
            kw["bias"] = bias
            if not isinstance(bias, (int, float)):
                ins.append(bias)
        if scale is not None:
            kw["scale"] = scale
            if not isinstance(scale, (int, float)):
                ins.append(scale)
        return self.op("act", lambda: nc.scalar.activation(out, in_, func, **kw), outs=[out], ins=ins)

    def tt(self, out, in0, in1, op, eng="dve"):
        e = self.eng[eng]
        return self.op(eng, lambda: e.tensor_tensor(out, in0, in1, op), outs=[out], ins=[in0, in1])

    def ts(self, out, in0, s1, s2, op0, op1=None, eng="dve"):
        e = self.eng[eng]
        ins = [in0] + [s for s in (s1, s2) if s is not None and not isinstance(s, (int, float))]
        if op1 is None:
            return self.op(eng, lambda: e.tensor_scalar(out, in0, s1, s2, op0), outs=[out], ins=ins)
        return self.op(eng, lambda: e.tensor_scalar(out, in0, s1, s2, op0, op1), outs=[out], ins=ins)

    def stt(self, out, in0, scalar, in1, op0, op1):
        nc = self.nc
        ins = [in0, in1] + ([] if isinstance(scalar, (int, float)) else [scalar])
        return self.op("dve", lambda: nc.vector.scalar_tensor_tensor(out, in0, scalar, in1, op0, op1),
                       outs=[out], ins=ins)

    def copy(self, out, in_, eng="dve"):
        nc = self.nc
        if eng == "act":
            return self.op("act", lambda: nc.scalar.copy(out, in_), outs=[out], ins=[in_])
        e = self.eng[eng]
        return self.op(eng, lambda: e.tensor_copy(out, in_), outs=[out], ins=[in_])

    def recip(self, out, in_):
        nc = self.nc
        return self.op("dve", lambda: nc.vector.reciprocal(out, in_), outs=[out], ins=[in_])

    def memset(self, out, val, eng="dve"):
        e = self.eng[eng]
        return self.op(eng, lambda: e.memset(out, val), outs=[out])

    def scan(self, out, d0, d1, init):
        nc = self.nc
        ins = [d0, d1] + ([] if isinstance(init, (int, float)) else [init])
        return self.op("dve", lambda: nc.vector.tensor_tensor_scan(out, d0, d1, init, ALU.mult, ALU.add),
                       outs=[out], ins=ins)


class Ring:
    def __init__(self, items):
        self.items = list(items)
        self.i = 0

    def next(self):
        v = self.items[self.i % len(self.items)]
        self.i += 1
        return v


class _Stop(Exception):
    pass


def build(nl, dbg_names=(), stop_after=None):
    nc = bass.Bass("TRN2", target_bir_lowering=False)
    dI = lambda name, shape: nc.dram_tensor(name, shape, F32, kind="ExternalInput").ap()
    dO = lambda name, shape: nc.dram_tensor(name, shape, F32, kind="ExternalOutput").ap()
    xT = dI("xT", [D, T])
    pv = dI("pv", [nl, 128, PVL])
    gl = dI("gl", [128, GLW])
    cst = dI("cst", [128, 5, 128])
    cs = dI("cs", [128, 4, 1024])
    xib = dI("xib", [128, 512])
    mq = dI("mq", [4, T])
    mk = dI("mk", [4, 1280])
    cckv = dI("cckv", [nl, 256, 256])
    ckr = dI("ckr", [nl, 64, 256])
    sret = dI("sret", [nl, 2, 8, 128, 128])
    w_mod = dI("w_mod", [nl, D, 6 * D])
    w_in = dI("w_in", [nl, D, N_IN])
    lru_wa = dI("lru_wa", [nl, 2, 8, 128, 128])
    lru_wx = dI("lru_wx", [nl, 2, 8, 128, 128])
    w_uq = dI("mla_wuq", [nl, 512, 1536])
    w_ukv = dI("mla_wukv", [nl, 256, 2048])
    w_br = [dI("w_br_lru", [nl, 1024, D]), dI("w_br_mla", [nl, 1024, D]), dI("w_br_ret", [nl, 1024, D])]
    w_out = dI("w_out", [nl, D, D])
    w_ff1 = dI("w_ff1", [nl, D, 4 * D])
    w_ff2 = dI("w_ff2", [nl, 4 * D, D])
    yT = dO("yT", [D, T])
    ckvo = dO("ckvo", [nl, 256, T])
    kro = dO("kro", [nl, 64, T])
    lruo = dO("lruo", [128, 256])
    reto = dO("reto", [nl, 4, 2, 8, 128, 128])
    dbg_out = {n: dO("dbg_" + n, shp) for n, shp in dbg_names}

    with ExitStack() as es:
        S = Sched(nc, es, ARENA)
        sb = S.sb
        X = sb(X_OFF, [128, 16, T])
        PV = sb(PV_OFF, [128, PVL])
        GL = sb(GL_OFF, [128, GLW])
        CSR = sb(CSR_OFF, [128, 2, T], BF16)
        CSM = sb(CSM_OFF, [128, 2, T], BF16)
        CB = sb(CB_OFF, [128, 4, 128], BF16)
        IDENT, ONES, RRET, RMLA = CB[:, 0, :], CB[:, 1, :], CB[:, 2, :], CB[:, 3, :]
        ONESF = sb(ONESF_OFF, [128, 128])
        XIB = sb(XIB_OFF, [128, 512])
        ADA = sb(ADA_OFF, [128, 96])
        DER = sb(DER_OFF, [128, 160])
        SILB = sb(SILB_OFF, [128, 16], BF16)
        LRUO = sb(LRUO_OFF, [128, 64])
        ADA_NEXT = sb(LRUO_OFF + 64, [128, 96])
        H = sb(H_OFF, [128, 16, T], BF16)
        U = sb(H_OFF, [128, 16, T], F32)
        YBR = sb(YBR_OFF, [128, 3, 8, T], BF16)
        CROSS = GL[:, GL_CROSS:GL_CROSS + 1]
        LM = GL[:, GL_LM:GL_LM + 1]
        PIDX = GL[:, GL_PIDX:GL_PIDX + 1]
        EPSC = DER[:, 159:160]
        ONEC = DER[:, 158:159]
        LNSC = DER[:, 157:158]

        def dbg(name, ap):
            if name in dbg_out:
                S.dma("pool" if ap.dtype != F32 else "sp", dbg_out[name], ap)

        def chk(name):
            if stop_after == name:
                raise _Stop()

        S.dma("sp", GL, gl)
        S.dma("sp", ONESF, cst[:, 4, :])
        S.dma("sp", XIB, xib)
        S.dma("pool", CB, cst[:, 0:4, :])
        S.dma("pool", CSR, cs[:, 0:2, :])
        S.dma("pool", CSM, cs[:, 2:4, :])
        S.dma("sp", X, xT.rearrange("(c p) t -> p c t", p=128))
        S.memset(LRUO, 0.0)
        S.memset(EPSC, EPS)
        S.memset(ONEC, 1.0)
        S.memset(LNSC, LNS)
        sil32 = sb(SCR_OFF, [128, 16])
        S.act(sil32, GL[:, GL_COND:GL_COND + 16], AF.Silu)
        S.copy(SILB, sil32)

        wslot = [0]

        def wbuf(shape, slot=None):
            if slot is None:
                slot = wslot[0] % 2
                wslot[0] += 1
            return sb(WB_OFF + slot * 2048, shape, BF16)

        def wview(w, l):
            return w[l].rearrange("(kc p) n -> p kc n", p=128)

        def rstd_from_bank(bank_ap, out, n_feat, tmp):
            S.act(tmp, bank_ap, AF.Sqrt, bias=EPSC, scale=1.0 / n_feat)
            S.recip(out, tmp)

        def pre_norm(acol, bcol, scr):
            sq = [sb(scr + i * 256, [128, TH], BF16) for i in range(2)]
            rstd = sb(scr + 512, [128, TH])
            tmpr = sb(scr + 1024, [128, TH])
            tm = [sb(scr + 1536 + i * 512, [128, TH]) for i in range(2)]
            for th in range(2):
                sl = slice(th * TH, (th + 1) * TH)
                bk = S.bank(6 + th)
                for fc in range(16):
                    s = sq[fc % 2]
                    S.act(s, X[:, fc, sl], AF.Square)
                    S.mm(bk, ONES, s, start=(fc == 0), stop=(fc == 15))
                rstd_from_bank(bk, rstd, D, tmpr)
                for fc in range(16):
                    t = tm[fc % 2]
                    S.stt(t, X[:, fc, sl], acol[:, fc:fc + 1], rstd, ALU.mult, ALU.mult)
                    S.act(H[:, fc, sl], t, AF.Identity, bias=bcol[:, fc:fc + 1])

        def post_norm_residual(src, gcol, scr, hook=None):
            rstd = sb(scr, [128, TH])
            tmpr = sb(scr + 512, [128, TH])
            tm = [sb(scr + 1024 + i * 512, [128, TH]) for i in range(2)]
            for th in range(2):
                sl = slice(th * TH, (th + 1) * TH)
                rstd_from_bank(S.bank(6 + th), rstd, D, tmpr)
                for fc in range(16):
                    t = tm[fc % 2]
                    S.stt(t, src[:, fc, sl], gcol[:, fc:fc + 1], rstd, ALU.mult, ALU.mult)
                    S.tt(X[:, fc, sl], X[:, fc, sl], t, ALU.add)
                    if hook is not None:
                        hook()

        tbank = Ring([0, 1, 2, 3])
        nb = lambda: S.bank(tbank.next())

        try:
          for l in range(nl):
              S.dma("sp", PV, pv[l])
              WIN = wview(w_in, l)
              if l == 0:
                  WMOD = wview(w_mod, l)
                  abank = S.bank(0)
                  for tl in range(48):
                      wt = wbuf([128, 16, 256])
                      S.dma("pool", wt, WMOD[:, :, tl * 256:(tl + 1) * 256])
                      for jj in range(2):
                          j = tl * 2 + jj
                          for kc in range(16):
                              S.mm(abank[:, j:j + 1], wt[:, kc, jj * 128:(jj + 1) * 128], SILB[:, kc:kc + 1],
                                   start=(kc == 0), stop=(kc == 15), sig=(kc == 15 and jj == 1))
                  S.tt(ADA, abank[:, 0:96], PV[:, PV_BMOD:PV_BMOD + 96], ALU.add)
              else:
                  S.tt(ADA, ADA_NEXT, PV[:, PV_BMOD:PV_BMOD + 96], ALU.add)
              SH1, SC1, GA1, SH2, SC2, GA2 = [ADA[:, i * 16:(i + 1) * 16] for i in range(6)]
              gn = lambda j: PV[:, PV_GN + 16 * j:PV_GN + 16 * (j + 1)]
              dcol = lambda o, n=16: DER[:, o:o + n]
              S.stt(dcol(DR_A1), SC1, 1.0, gn(0), ALU.add, ALU.mult)
              S.tt(dcol(DR_GA1), GA1, gn(1), ALU.mult)
              S.stt(dcol(DR_A2), SC2, 1.0, gn(2), ALU.add, ALU.mult)
              S.tt(dcol(DR_GA2), GA2, gn(3), ALU.mult)
              S.act(dcol(DR_TMP), PV[:, PV_LAM:PV_LAM + 16], AF.Exp, scale=-1.0)
              S.act(dcol(DR_TMP), dcol(DR_TMP), AF.Ln, bias=ONEC)
              S.ts(dcol(DR_KL), dcol(DR_TMP), -8.0, None, ALU.mult)
              S.ts(dcol(DR_K2), dcol(DR_TMP), -16.0, None, ALU.mult)
              S.act(dcol(DR_NLG), PV[:, PV_DEC:PV_DEC + 16], AF.Exp, scale=-1.0)
              S.act(dcol(DR_NLG), dcol(DR_NLG), AF.Ln, bias=ONEC)
              S.ts(dcol(DR_LG), dcol(DR_NLG), -1.0, None, ALU.mult)
              if l == 0:
                  dbg("ada0", ADA)
              if l == 1:
                  dbg("ada1", ADA)
                  dbg("der0", DER)
              chk("ada")

              pre_norm(dcol(DR_A1), SH1, SCR_OFF)
              if l == 0:
                  dbg("h0", H)
              chk("prenorm")

              RS = YBR_OFF
              E2 = [sb(RS + i * 960, [128, 1920], BF16) for i in range(2)]
              RQ = sb(RS + 1920, [128, T], BF16)
              RK = sb(RS + 2432, [128, T], BF16)
              RV = sb(RS + 2944, [128, 8, 128], BF16)
              SRG = sb(RS + 3456, [128, T], BF16)
              S0 = sb(RS + 3968, [128, 16, 128], BF16)
              KD = [sb(RS + 4992 + i * 512, [128, 8, 128], BF16) for i in range(2)]
              STG = sb(RS + 6016, [128, 4, 2, 128])
              TB = [sb(SCR_OFF + i * 512, [128, TH]) for i in range(6)]
              TBB = [sb(SCR_OFF + 3072 + i * 256, [128, TH], BF16) for i in range(7)]
              BCOL = sb(SCR_OFF + 4864, [128, 16])
              WDEC = sb(SCR_OFF + 4880, [128, 4])
              DFB = sb(SCR_OFF + 4884, [128, 4])
              XBAND = sb(SCR_OFF + 4896, [128, 128])
              rslots = [WB_OFF, WB_OFF + 2048, SCR_OFF + 5120]
              rwi = [0]

              def rwbuf():
                  off = rslots[rwi[0] % 3]
                  rwi[0] += 1
                  return sb(off, [128, 16, 256], BF16)

              ada_bufs = [sb(SCR_OFF + 7168, [128, 8, 256], BF16), sb(RS + 7040, [128, 8, 256], BF16)]
              ada_i = [0]
              do_ada = (l + 1 < nl)
              abank6 = S.bank(6)
              if do_ada:
                  WMODN = wview(w_mod, l + 1)
                  S.dma("pool", ada_bufs[0], WMODN[:, 0:8, 0:256])

              def ada_tile_ap(i):
                  j2, kh = i // 2, i % 2
                  return WMODN[:, kh * 8:(kh + 1) * 8, j2 * 256:(j2 + 1) * 256]

              ada_calls = [0]

              def ada_step(n):
                  if not do_ada:
                      return
                  if n == 1:
                      ada_calls[0] += 1
                      if ada_calls[0] % 2 == 0:
                          return
                  for _ in range(n):
                      i = ada_i[0]
                      if i >= 48:
                          return
                      if i + 1 < 48:
                          S.dma("pool", ada_bufs[(i + 1) % 2], ada_tile_ap(i + 1))
                      bf = ada_bufs[i % 2]
                      j2, kh = i // 2, i % 2
                      for jj in range(2):
                          col = S.bank(6 + jj)[:, j2:j2 + 1]
                          for kc in range(8):
                              S.mm(col, bf[:, kc, jj * 128:(jj + 1) * 128], SILB[:, kh * 8 + kc:kh * 8 + kc + 1],
                                   start=(kh == 0 and kc == 0), stop=(kh == 1 and kc == 7), sig=(kc == 7 and jj == 1))
                      ada_i[0] += 1

              def ret_wdma(hd_):
                  wqk_ = rwbuf()
                  S.dma("pool", wqk_[:, :, 0:128], WIN[:, :, C_RQ + hd_ * 128:C_RQ + (hd_ + 1) * 128])
                  S.dma("pool", wqk_[:, :, 128:256], WIN[:, :, C_RK + hd_ * 128:C_RK + (hd_ + 1) * 128])
                  return wqk_

              def ret_wdma2(hd_):
                  wvg_ = rwbuf()
                  S.dma("pool", wvg_[:, :, 0:128], WIN[:, :, C_RV + hd_ * 128:C_RV + (hd_ + 1) * 128])
                  S.dma("pool", wvg_[:, :, 128:256], WIN[:, :, C_RG + hd_ * 128:C_RG + (hd_ + 1) * 128])
                  return wvg_

              nxt_qk = ret_wdma(0)
              nxt_vg = ret_wdma2(0)
              S.dma("pool", S0, sret[l].rearrange("d h k v -> k (d h) v"))
              S.ts(XBAND, XIB[:, 0:128], PIDX, None, ALU.subtract)
              rbank = Ring([0, 1, 2, 3])
              rb = lambda: S.bank(rbank.next())
              for hd in range(8):
                  wqk, wvg = nxt_qk, nxt_vg
                  if hd + 1 < 8:
                      nxt_qk = ret_wdma(hd + 1)
                  chk("ret_a1")
                  LGF = dcol(DR_LG + hd, 1)
                  LGB = dcol(DR_LG + 8 + hd, 1)
                  NLGB = dcol(DR_NLG + 8 + hd, 1)
                  for th in range(2):
                      sl = slice(th * TH, (th + 1) * TH)
                      pbk = []
                      for which in range(2):
                          bk = rb()
                          for kc in range(16):
                              S.mm(bk, wqk[:, kc, which * 128:(which + 1) * 128], H[:, kc, sl], start=(kc == 0), stop=(kc == 15))
                          xb = TBB[0] if which == 0 else TBB[6]
                          S.copy(xb, bk, eng="act")
                          pbk.append((bk, xb))
                      for which, dst in ((0, RQ), (1, RK)):
                          bk, xb = pbk[which]
                          b2 = rb()
                          S.mm(b2, RRET, xb, start=True, stop=True)
                          t0_, t1_ = (TB[0], TB[1]) if which == 0 else (TB[3], TB[4])
                          S.tt(t0_, bk, CSR[:, 0, sl], ALU.mult)
                          S.tt(t1_, b2, CSR[:, 1, sl], ALU.mult)
                          S.tt(dst[:, sl], t0_, t1_, ALU.add)
                      if th == 1 and hd + 1 < 8:
                          nxt_vg = ret_wdma2(hd + 1)
                      ada_step(1)
                  S.ts(BCOL[:, 0:3], GL[:, GL_PSH:GL_PSH + 3], NLGB, LNS, ALU.mult, ALU.add)
                  S.ts(BCOL[:, 3:6], GL[:, GL_PSH + 3:GL_PSH + 6], LGF, LNS, ALU.mult, ALU.add)
                  S.ts(BCOL[:, 6:12], BCOL[:, 0:6], LM, None, ALU.add)
                  S.ts(TB[2][:, 0:128], XBAND, LGF, 0.0, ALU.mult, ALU.min)
                  S.ts(TB[2][:, 128:256], XBAND, NLGB, 0.0, ALU.mult, ALU.min)
                  S.tt(TB[2][:, 0:128], TB[2][:, 0:128], TB[2][:, 128:256], ALU.add)
                  for th in range(2):
                      S.ts(DFB[:, 2 * th:2 * th + 1], LGF, float(th * 512 + 1), None, ALU.mult)
                      S.ts(DFB[:, 2 * th + 1:2 * th + 2], NLGB, float(th * 512 - 1024), None, ALU.mult)
                  S.act(TB[2][:, 256:384], TB[2][:, 0:128], AF.Exp, bias=LNSC)
                  for par in range(2):
                      E = E2[par]
                      S.stt(E[:, 896:1024], IDENT, SK, TB[2][:, 256:384], ALU.mult, ALU.add)
                      inseg = (1024, 1280) if par == 0 else (896, 1152)
                      for bi, (c0, c1) in enumerate(EBLK):
                          masked = not (c0 >= inseg[0] and c1 <= inseg[1])
                          sc = NLGB if bi < 3 else LGF
                          bcol_i = bi + (6 if masked else 0)
                          S.act(E[:, c0 - 128:c1 - 128], XIB[:, 0:c1 - c0], AF.Exp, bias=BCOL[:, bcol_i:bcol_i + 1], scale=sc)
                  S.act(WDEC[:, 0:2], GL[:, GL_POSF:GL_POSF + 2], AF.Exp, bias=LNSC, scale=LGF)
                  S.act(WDEC[:, 2:4], GL[:, GL_POSB:GL_POSB + 2], AF.Exp, bias=LNSC, scale=LGB)
                  qd = {}
                  for th in range(2):
                      sl = slice(th * TH, (th + 1) * TH)
                      S.act(TB[3], XIB, AF.Exp, bias=DFB[:, 2 * th:2 * th + 1], scale=LGF)
                      S.act(TB[4], XIB, AF.Exp, bias=DFB[:, 2 * th + 1:2 * th + 2], scale=NLGB)
                      qdf, qdb = (TBB[1], TBB[2]) if th == 0 else (TBB[0], TBB[6])
                      S.tt(qdf, RQ[:, sl], TB[3], ALU.mult)
                      S.tt(qdb, RQ[:, sl], TB[4], ALU.mult)
                      qd[th] = (qdf, qdb)
                  for th in range(2):
                      sl = slice(th * TH, (th + 1) * TH)
                      bk = rb()
                      for kc in range(16):
                          S.mm(bk, wvg[:, kc, 128:256], H[:, kc, sl], start=(kc == 0), stop=(kc == 15))
                      S.act(SRG[:, sl], bk, AF.Silu)
                      ada_step(1)
                  chk("ret_a3")
                  for g in range(2):
                      bk = rb()
                      for j in range(4):
                          tt_ = g * 4 + j
                          for kc in range(16):
                              S.mm(bk[:, j * 128:(j + 1) * 128], H[:, kc, tt_ * 128:(tt_ + 1) * 128], wvg[:, kc, 0:128],
                                   start=(kc == 0), stop=(kc == 15), sig=(kc == 15 and j == 3))
                      S.copy(RV[:, g * 4:(g + 1) * 4, :], bk.rearrange("p (a b) -> p a b", a=4), eng="act")
                      ada_step(1)
                  chk("ret_a")
                  kbank = S.bank(rbank.next(), dt=BF16)
                  for tt_ in range(8):
                      S.tr(kbank[:, tt_ * 128:(tt_ + 1) * 128], RK[:, tt_ * 128:(tt_ + 1) * 128], IDENT, sig=(tt_ == 7))
                  kb3 = kbank.rearrange("p (a b c) -> p a b c", a=4, b=2)
                  for dr in range(2):
                      kd3 = KD[dr].rearrange("p (a b) c -> p a b c", b=2)
                      for par in range(2):
                          S.ts(kd3[:, :, par, :], kb3[:, :, par, :], WDEC[:, dr * 2 + par:dr * 2 + par + 1], None, ALU.mult)
                  stb = [rb(), rb()]
                  for seg in range(4):
                      for dr in range(2):
                          idx = seg * 2 + dr
                          ob = stb[idx // 4][:, (idx % 4) * 128:(idx % 4 + 1) * 128]
                          for c2 in range(2):
                              S.mm(ob, KD[dr][:, seg * 2 + c2, :], RV[:, seg * 2 + c2, :], start=(c2 == 0), stop=(c2 == 1),
                                   sig=(c2 == 1 and idx % 4 == 3))
                  stg2 = STG.rearrange("p a b c -> p (a b) c")
                  S.copy(stg2[:, 0:4, :], stb[0].rearrange("p (a b) -> p a b", a=4), eng="act")
                  S.copy(stg2[:, 4:8, :], stb[1].rearrange("p (a b) -> p a b", a=4), eng="act")
                  S.dma("sp", reto[l][:, :, hd].rearrange("s d k v -> k s d v"), STG)
                  chk("ret_b")
                  for th in range(2):
                      sl = slice(th * TH, (th + 1) * TH)
                      qdf, qdb = qd[th]
                      obank = S.bank(4 + th)
                      sbks = {}

                      def emit_scores(si_):
                          sbks[si_] = rb()
                          S.mm(sbks[si_], RK[:, si_ * 128:(si_ + 1) * 128], RQ[:, sl], start=True, stop=True)

                      emit_scores(0)
                      emit_scores(1)
                      for si in range(8):
                          if si + 2 < 8:
                              emit_scores(si + 2)
                          sbk = sbks[si]
                          pT = TBB[3 + (si % 3)]
                          off = th * 512 - 128 * si + 896
                          S.tt(pT, sbk, E2[si % 2][:, off:off + 512], ALU.mult)
                          S.mm(obank, RV[:, si, :], pT, start=(si == 0), stop=False, sig=False)
                          if si in (1, 4, 6):
                              ada_step(1)
                      S.mm(obank, S0[:, hd, :], qdf, start=False, stop=False, sig=False)
                      S.mm(obank, S0[:, 8 + hd, :], qdb, start=False, stop=True)
                  for th in range(2):
                      sl = slice(th * TH, (th + 1) * TH)
                      obank = S.bank(4 + th)
                      o32, osq = TB[0], TB[1]
                      S.copy(o32, obank, eng="act")
                      S.act(osq, obank, AF.Square)
                      mbank = rb()
                      vbank = rb()
                      S.mm(mbank, ONESF, o32, start=True, stop=True)
                      S.mm(vbank, ONESF, osq, start=True, stop=True)
                      S.act(TB[5], mbank, AF.Square)
                      S.tt(o32, o32, mbank, ALU.subtract)
                      S.tt(TB[5], vbank, TB[5], ALU.subtract)
                      S.ts(TB[5], TB[5], 0.0, None, ALU.max)
                      S.act(TB[5], TB[5], AF.Sqrt, bias=EPSC)
                      S.recip(TB[1], TB[5])
                      S.tt(o32, o32, TB[1], ALU.mult)
                      S.tt(YBR[:, 2, hd, sl], o32, SRG[:, sl], ALU.mult)
              if do_ada:
                  ada_step(96)
                  an3 = ADA_NEXT[:, 0:48].rearrange("p (a b) -> p a b", b=2)
                  S.copy(an3[:, :, 0], S.bank(6)[:, 0:24])
                  S.copy(an3[:, :, 1], S.bank(7)[:, 0:24])
              if l == 0:
                  dbg("oret0", YBR[:, 2])
              chk("ret")

              MS = YBR_OFF
              CQN = sb(MS, [128, 4, T], BF16)
              CKVA = sb(MS + 2048, [128, 2, 1280], BF16)
              KRX = sb(MS + 3328, [128, 1280], BF16)
              M0 = SCR_OFF
              CQ32 = sb(M0, [128, 4, T])
              CKV32 = sb(M0 + 4096, [128, 2, T])
              KR32 = sb(M0, [64, T])
              NRS = sb(M0 + 6144, [128, TH])
              NTM = sb(M0 + 6656, [128, TH])
              NSQ = sb(M0 + 7168, [128, TH], BF16)
              KNT = sb(M0 + 1024, [128, 1280], BF16)
              VH = sb(M0 + 1664, [128, 10, 128], BF16)
              QNT = sb(M0 + 2304, [128, T], BF16)
              QRX = sb(M0 + 2816, [128, T], BF16)
              WQ = sb(M0 + 3328, [128, 4, 768], BF16)
              MT = [sb(M0 + 4864 + i * 512, [128, TH]) for i in range(3)]
              MTB = [sb(M0 + 6400 + i * 256, [128, TH], BF16) for i in range(4)]
              S.dma("pool", CKVA[:, :, 0:256], cckv[l].rearrange("(rc p) s -> p rc s", p=128))
              S.dma("pool", sb(MS + 3328, [64, 1280], BF16)[:, 0:256], ckr[l])
              S.dma("pool", sb(MS + 3328, [4, 1280], BF16, p0=64), mk)

              def small_norm(nch, c0, dst32, nfeat, emit):
                  for tl in range((nch + 1) // 2):
                      wt_ = wbuf([128, 16, 256])
                      S.dma("pool", wt_, WIN[:, :, c0 + tl * 256:c0 + (tl + 1) * 256])
                      for jj in range(2):
                          j = tl * 2 + jj
                          for th in range(2):
                              sl = slice(th * TH, (th + 1) * TH)
                              bk = nb()
                              for kc in range(16):
                                  S.mm(bk, wt_[:, kc, jj * 128:(jj + 1) * 128], H[:, kc, sl], start=(kc == 0), stop=(kc == 15))
                              S.copy(dst32[:, j, sl], bk, eng="act")
                              S.act(NSQ, bk, AF.Square)
                              S.mm(S.bank(6 + th), ONES, NSQ, start=(j == 0), stop=(j == nch - 1))
                  for th in range(2):
                      sl = slice(th * TH, (th + 1) * TH)
                      rstd_from_bank(S.bank(6 + th), NRS, nfeat, NTM)
                      for j in range(nch):
                          emit(j, th, sl, NRS)

              def emit_q(j, th, sl, rstd):
                  S.stt(CQN[:, j, sl], CQ32[:, j, sl], PV[:, PV_GQ + j:PV_GQ + j + 1], rstd, ALU.mult, ALU.mult)

              small_norm(4, C_CQ, CQ32, 512, emit_q)

              def emit_kv(j, th, sl, rstd):
                  S.stt(CKV32[:, j, sl], CKV32[:, j, sl], PV[:, PV_GKV + j:PV_GKV + j + 1], rstd, ALU.mult, ALU.mult)
                  S.copy(CKVA[:, j, 256 + th * TH:256 + (th + 1) * TH], CKV32[:, j, sl])

              small_norm(2, C_CKV, CKV32, 256, emit_kv)
              S.dma("sp", ckvo[l].rearrange("(rc p) t -> p rc t", p=128), CKV32)
              wt = wbuf([128, 16, 64])
              S.dma("pool", wt, WIN[:, :, C_KR:C_KR + 64])
              for th in range(2):
                  sl = slice(th * TH, (th + 1) * TH)
                  bk = nb()
                  for kc in range(16):
                      S.mm(bk[0:64, :], wt[:, kc, :], H[:, kc, sl], start=(kc == 0), stop=(kc == 15))
                  S.copy(KR32[:, sl], bk[0:64, :], eng="act")
                  xb = MTB[0]
                  S.copy(xb[0:64, :], bk[0:64, :], eng="act")
                  b2 = nb()
                  S.mm(b2[0:64, :], RMLA[0:64, 0:64], xb[0:64, :], start=True, stop=True)
                  S.tt(MT[0][0:64, :], bk[0:64, :], CSM[0:64, 0, sl], ALU.mult)
                  S.tt(MT[1][0:64, :], b2[0:64, :], CSM[0:64, 1, sl], ALU.mult)
                  S.tt(KRX[0:64, 256 + th * TH:256 + (th + 1) * TH], MT[0][0:64, :], MT[1][0:64, :], ALU.add)
              S.dma("sp", kro[l], KR32)
              wkv = wbuf([128, 2, 2048])
              S.dma("pool", wkv, w_ukv[l].rearrange("(rc p) n -> p rc n", p=128))
              S.dma("pool", sb(M0 + 2816, [4, T], BF16, p0=64), mq)
              for hd in range(8):
                  if hd % 4 == 0:
                      S.dma("pool", WQ, w_uq[l].rearrange("(kc p) n -> p kc n", p=128)[:, :, (hd // 4) * 768:(hd // 4 + 1) * 768])
                  hq = (hd % 4) * 192
                  for (s0, s1) in ((0, 512), (512, 1024), (1024, 1280)):
                      bk = nb()
                      for rc in range(2):
                          S.mm(bk[:, 0:s1 - s0], wkv[:, rc, hd * 256:hd * 256 + 128], CKVA[:, rc, s0:s1], start=(rc == 0), stop=(rc == 1))
                      S.copy(KNT[:, s0:s1], bk[:, 0:s1 - s0], eng="act")
                  for (a0, a1) in ((0, 4), (4, 8), (8, 10)):
                      bk = nb()
                      for sc_ in range(a0, a1):
                          for rc in range(2):
                              S.mm(bk[:, (sc_ - a0) * 128:(sc_ - a0 + 1) * 128], CKVA[:, rc, sc_ * 128:(sc_ + 1) * 128],
                                   wkv[:, rc, hd * 256 + 128:hd * 256 + 256], start=(rc == 0), stop=(rc == 1),
                                   sig=(rc == 1 and sc_ == a1 - 1))
                      n = a1 - a0
                      S.copy(VH[:, a0:a1, :], bk[:, 0:n * 128].rearrange("p (a b) -> p a b", a=n), eng="act")
                  for th in range(2):
                      sl = slice(th * TH, (th + 1) * TH)
                      bk = nb()
                      for kc in range(4):
                          S.mm(bk, WQ[:, kc, hq:hq + 128], CQN[:, kc, sl], start=(kc == 0), stop=(kc == 3))
                      S.copy(QNT[:, sl], bk, eng="act")
                      bk = nb()
                      for kc in range(4):
                          S.mm(bk[0:64, :], WQ[:, kc, hq + 128:hq + 192], CQN[:, kc, sl], start=(kc == 0), stop=(kc == 3))
                      xb = MTB[0]
                      S.copy(xb[0:64, :], bk[0:64, :], eng="act")
                      b2 = nb()
                      S.mm(b2[0:64, :], RMLA[0:64, 0:64], xb[0:64, :], start=True, stop=True)
                      S.tt(MT[0][0:64, :], bk[0:64, :], CSM[0:64, 0, sl], ALU.mult)
                      S.tt(MT[1][0:64, :], b2[0:64, :], CSM[0:64, 1, sl], ALU.mult)
                      S.tt(QRX[0:64, sl], MT[0][0:64, :], MT[1][0:64, :], ALU.add)
                  for th in range(2):
                      sl = slice(th * TH, (th + 1) * TH)
                      obank = S.bank(4 + 2 * th)
                      dbank = S.bank(5 + 2 * th)
                      sbks = {}

                      def emit_sc(s_):
                          sbks[s_] = nb()
                          S.mm(sbks[s_], KNT[:, s_ * 128:(s_ + 1) * 128], QNT[:, sl], start=True, stop=False, sig=False)
                          S.mm(sbks[s_], KRX[0:68, s_ * 128:(s_ + 1) * 128], QRX[0:68, sl], start=False, stop=True)

                      emit_sc(0)
                      emit_sc(1)
                      for sc_ in range(10):
                          if sc_ + 2 < 10:
                              emit_sc(sc_ + 2)
                          sbk = sbks[sc_]
                          pT = MTB[1 + (sc_ % 3)]
                          S.act(pT, sbk, AF.Exp, scale=MLA_SCALE)
                          S.mm(obank, VH[:, sc_, :], pT, start=(sc_ == 0), stop=(sc_ == 9))
                          S.mm(dbank, ONES, pT, start=(sc_ == 0), stop=(sc_ == 9))
                      S.recip(MT[2], dbank)
                      S.tt(YBR[:, 1, hd, sl], obank, MT[2], ALU.mult)
              if l == 0:
                  dbg("omla0", YBR[:, 1])
              chk("mla")

              L0 = SCR_OFF
              XR = sb(L0, [128, 4, 262], BF16)
              DG = sb(L0 + 524, [128, 4, 128], BF16)
              GG = sb(L0 + 1048, [128, T], BF16)
              XC = sb(L0 + 1560, [128, 4, 256])
              XCB = sb(L0 + 2584, [128, T], BF16)
              RR = sb(L0 + 3096, [128, T])
              IG = sb(L0 + 4120, [128, T])
              A2 = sb(L0 + 5144, [128, T])
              HSF = sb(L0 + 6168, [128, T])
              HSB = sb(L0 + 7192, [128, T])
              assert 8216 <= SCR_W
              wax_slot = wslot[0] % 2
              wslot[0] += 1
              wax = wbuf([128, 32, 128], slot=wax_slot)
              S.dma("pool", wax[:, 0:16, :], lru_wa[l].rearrange("d n c e -> c (d n) e"))
              S.dma("pool", wax[:, 16:32, :], lru_wx[l].rearrange("d n c e -> c (d n) e"))
              S.memset(XR[:, 0, 0:3], 0.0)
              S.memset(XR[:, 3, 259:262], 0.0)
              xcf = XC.rearrange("p a b -> p (a b)")
              lbank = Ring(range(8))
              nb_saved = nb
              nb = lambda: S.bank(lbank.next())
              for c in range(8):
                  wt = wbuf([128, 16, 256], slot=1 - wax_slot)
                  S.dma("pool", wt[:, :, 0:128], WIN[:, :, c * 128:(c + 1) * 128])
                  S.dma("pool", wt[:, :, 128:256], WIN[:, :, 1024 + c * 128:1024 + (c + 1) * 128])
                  for th in range(2):
                      sl = slice(th * TH, (th + 1) * TH)
                      bk = nb()
                      for kc in range(16):
                          S.mm(bk, wt[:, kc, 0:128], H[:, kc, sl], start=(kc == 0), stop=(kc == 15))
                      S.copy(XR[:, 2 * th:2 * th + 2, 3:259], bk.rearrange("p (a b) -> p a b", a=2), eng="act")
                      bk = nb()
                      for kc in range(16):
                          S.mm(bk, wt[:, kc, 128:256], H[:, kc, sl], start=(kc == 0), stop=(kc == 15))
                      S.act(GG[:, sl], bk, AF.Gelu)
                  S.ts(XR[:, 1:4, 0:3], XR[:, 0:3, 256:259], CROSS, None, ALU.mult)
                  S.ts(XR[:, 0:3, 259:262], XR[:, 1:4, 3:6], CROSS, None, ALU.mult)
                  for dr in range(2):
                      cw = lambda j: PV[:, PV_CW + (dr * 4 + j) * 8 + c:PV_CW + (dr * 4 + j) * 8 + c + 1]
                      pcol = lambda base: PV[:, base + dr * 8 + c:base + dr * 8 + c + 1]
                      for j in range(4):
                          S.ts(DG[:, j, :], IDENT, cw(j), None, ALU.mult)
                      for th in range(2):
                          sl = slice(th * TH, (th + 1) * TH)
                          bk = nb()
                          for j in range(4):
                              o_ = j if dr == 0 else 6 - j
                              S.mm(bk, DG[:, j, :], XR[:, 2 * th:2 * th + 2, o_:o_ + 256], start=(j == 0), stop=(j == 3))
                          S.act(XCB[:, sl], bk, AF.Identity, bias=pcol(PV_CB))
                          S.ts(xcf[:, sl], bk, pcol(PV_CB), None, ALU.add)
                      for th in range(2):
                          sl = slice(th * TH, (th + 1) * TH)
                          bk = nb()
                          S.mm(bk, wax[:, dr * 8 + c, :], XCB[:, sl], start=True, stop=True)
                          S.act(RR[:, sl], bk, AF.Sigmoid, bias=pcol(PV_BA))
                          bk = nb()
                          S.mm(bk, wax[:, 16 + dr * 8 + c, :], XCB[:, sl], start=True, stop=True)
                          S.act(IG[:, sl], bk, AF.Sigmoid, bias=pcol(PV_BX))
                      kl = dcol(DR_KL + dr * 8 + c, 1)
                      k2 = dcol(DR_K2 + dr * 8 + c, 1)
                      S.act(A2, RR, AF.Exp, scale=k2)
                      S.act(RR, RR, AF.Exp, scale=kl)
                      S.ts(A2, A2, 1.0, None, ALU.min)
                      S.act(A2, A2, AF.Sqrt, bias=ONEC, scale=-1.0)
                      S.tt(IG, IG, xcf, ALU.mult)
                      S.tt(IG, IG, A2, ALU.mult)
                      a3 = RR.rearrange("p (a b) -> p a b", a=4)
                      h0 = PV[:, PV_H0 + dr * 8 + c:PV_H0 + dr * 8 + c + 1]
                      lo = LRUO.rearrange("p (d c s) -> p d c s", d=2, c=8)[:, dr, c, :]
                      if dr == 0:
                          S.ts(a3[:, 1:4, 0:1], a3[:, 1:4, 0:1], CROSS, None, ALU.mult)
                          S.scan(HSF, RR, IG, h0)
                          S.copy(lo, HSF.rearrange("p (a b) -> p a b", a=4)[:, :, 255])
                      else:
                          S.ts(a3[:, 0:3, 255:256], a3[:, 0:3, 255:256], CROSS, None, ALU.mult)
                          S.scan(HSB[:, ::-1], RR[:, ::-1], IG[:, ::-1], h0)
                          S.copy(lo, HSB.rearrange("p (a b) -> p a b", a=4)[:, :, 0])
                  S.tt(HSF, HSF, HSB, ALU.add)
                  S.tt(YBR[:, 0, c, :], HSF, GG, ALU.mult)
              S.dma("sp", lruo[:, l * 64:(l + 1) * 64], LRUO)
              nb = nb_saved
              if l == 0:
                  dbg("ylru0", YBR[:, 0])
              chk("lru")

              MG = sb(SCR_OFF, [128, 16, T], BF16)
              mbank = Ring([0, 1, 2, 3, 6, 7])
              nb = lambda: S.bank(mbank.next())
              assert SCR_W >= 8192
              SG = sb(WB_OFF + 1536, [128, TH])
              MA = sb(WB_OFF + 2048 + 1536, [128, TH])
              for fc in range(16):
                  for b in range(3):
                      slot = wslot[0] % 2
                      wslot[0] += 1
                      wg = sb(WB_OFF + slot * 2048, [128, 16, 128], BF16)
                      wb_ = sb(WB_OFF + slot * 2048 + 1024, [128, 8, 128], BF16)
                      c0 = C_G + b * 2048 + fc * 128
                      S.dma("pool", wg, WIN[:, :, c0:c0 + 128])
                      S.dma("pool", wb_, w_br[b][l].rearrange("(kc p) n -> p kc n", p=128)[:, :, fc * 128:(fc + 1) * 128])
                      for th in range(2):
                          sl = slice(th * TH, (th + 1) * TH)
                          gb = nb()
                          for kc in range(16):
                              S.mm(gb, wg[:, kc, :], H[:, kc, sl], start=(kc == 0), stop=(kc == 15))
                          ub = nb()
                          for kc in range(8):
                              S.mm(ub, wb_[:, kc, :], YBR[:, b, kc, sl], start=(kc == 0), stop=(kc == 7))
                          S.act(SG, gb, AF.Sigmoid)
                          acc = S.bank(4 + th)
                          if b == 0:
                              S.tt(acc, ub, SG, ALU.mult)
                          elif b == 1:
                              S.tt(MA, ub, SG, ALU.mult)
                              S.tt(acc, acc, MA, ALU.add)
                          else:
                              S.tt(MA, ub, SG, ALU.mult)
                              S.tt(MG[:, fc, sl], acc, MA, ALU.add)
              nb = nb_saved
              if l == 0:
                  dbg("mg0", MG)
              chk("merge")
              WO = wview(w_out, l)
              do_ada2 = (l + 1 < nl)
              ada2_i = [0]
              if do_ada2:
                  WMODN2 = wview(w_mod, l + 1)
                  ada2_bufs = [sb(YBR_OFF + 8192 + i * 2048, [128, 16, 256], BF16) for i in range(2)]
                  abank4 = S.bank(4)
                  S.dma("pool", ada2_bufs[0], WMODN2[:, :, 24 * 256:25 * 256])

              def ada2_step(n=1):
                  if not do_ada2:
                      return
                  for _ in range(n):
                      i = ada2_i[0]
                      if i >= 24:
                          return
                      if i + 1 < 24:
                          S.dma("pool", ada2_bufs[(i + 1) % 2], WMODN2[:, :, (25 + i) * 256:(26 + i) * 256])
                      bf = ada2_bufs[i % 2]
                      for jj in range(2):
                          j = 2 * i + jj
                          for kc in range(16):
                              S.mm(abank4[:, j:j + 1], bf[:, kc, jj * 128:(jj + 1) * 128], SILB[:, kc:kc + 1],
                                   start=(kc == 0), stop=(kc == 15), sig=(kc == 15 and jj == 1))
                      ada2_i[0] += 1
              sqb = [sb(WB_OFF + 1536 + i * 2048, [128, TH], BF16) for i in range(2)]
              obank_ring = Ring([0, 1, 2, 3, 5])
              nb = lambda: S.bank(obank_ring.next())
              for fc2 in range(16):
                  slot = wslot[0] % 2
                  wslot[0] += 1
                  wt = sb(WB_OFF + slot * 2048, [128, 16, 128], BF16)
                  S.dma("pool", wt, WO[:, :, fc2 * 128:(fc2 + 1) * 128])
                  for th in range(2):
                      sl = slice(th * TH, (th + 1) * TH)
                      bk = nb()
                      for kc in range(16):
                          S.mm(bk, wt[:, kc, :], MG[:, kc, sl], start=(kc == 0), stop=(kc == 15))
                      S.copy(U[:, fc2, sl], bk, eng="act")
                      s = sqb[th]
                      S.act(s, bk, AF.Square)
                      S.mm(S.bank(6 + th), ONES, s, start=(fc2 == 0), stop=(fc2 == 15))
                  ada2_step(1)
              if l == 0:
                  dbg("u0", U)
              nb = nb_saved
              post_norm_residual(U, dcol(DR_GA1), SCR_OFF, hook=ada2_step)
              if l == 0:
                  dbg("x0a", X)
              chk("wout")

              FB = YBR_OFF
              YACC = sb(FB, [128, 16, T])
              AG = [sb(FB + 16384 + i * 2048, [128, 4, T], BF16) for i in range(2)]
              FW = FB + 16384 + 4096
              assert FW + 4096 <= ARENA
              pre_norm(dcol(DR_A2), SH2, FB)
              if do_ada2:
                  ada2_step(24)
                  S.copy(ADA_NEXT[:, 48:96], abank4[:, 0:48])
              W1 = wview(w_ff1, l)
              fbank = Ring([0, 1, 2, 3, 4, 5])
              nb = lambda: S.bank(fbank.next())
              fslot = 0
              for g in range(16):
                  ag = AG[g % 2]
                  for t2 in range(2):
                      wt = sb(FW + (fslot % 2) * 2048, [128, 16, 256], BF16)
                      fslot += 1
                      j0 = g * 4 + t2 * 2
                      S.dma("pool", wt, W1[:, :, j0 * 128:(j0 + 2) * 128])
                      for jj in range(2):
                          for th in range(2):
                              sl = slice(th * TH, (th + 1) * TH)
                              bk = nb()
                              for kc in range(16):
                                  S.mm(bk, wt[:, kc, jj * 128:(jj + 1) * 128], H[:, kc, sl], start=(kc == 0), stop=(kc == 15))
                              dst = ag[:, t2 * 2 + jj, sl]
                              S.act(dst, bk, AF.Relu)
                              S.stt(dst, bk, 0.0, dst, ALU.max, ALU.mult)
                  for q4 in range(4):
                      wt = sb(FW + (fslot % 2) * 2048, [128, 4, 512], BF16)
                      fslot += 1
                      S.dma("pool", wt, w_ff2[l][g * 512:(g + 1) * 512, :].rearrange("(jc p) n -> p jc n", p=128)[:, :, q4 * 512:(q4 + 1) * 512])
                      for f4 in range(4):
                          fc2 = q4 * 4 + f4
                          for th in range(2):
                              sl = slice(th * TH, (th + 1) * TH)
                              bk = nb()
                              for jc in range(4):
                                  S.mm(bk, wt[:, jc, f4 * 128:(f4 + 1) * 128], ag[:, jc, sl], start=(jc == 0), stop=(jc == 3))
                              if g == 0:
                                  S.copy(YACC[:, fc2, sl], bk, eng="act")
                              else:
                                  S.tt(YACC[:, fc2, sl], YACC[:, fc2, sl], bk, ALU.add)
              zs = [sb(FB + 16384 + 2048 + i * 256, [128, TH], BF16) for i in range(2)]
              for th in range(2):
                  sl = slice(th * TH, (th + 1) * TH)
                  for fc in range(16):
                      s = zs[fc % 2]
                      S.act(s, YACC[:, fc, sl], AF.Square)
                      S.mm(S.bank(6 + th), ONES, s, start=(fc == 0), stop=(fc == 15))
              nb = nb_saved
              post_norm_residual(YACC, dcol(DR_GA2), FB + 16384)
              if l == 0:
                  dbg("x0b", X)
              chk("ffn")
        except (_Stop, _StopOps):
            S.stop_at = None

        S.dma("sp", yT.rearrange("(c p) t -> p c t", p=128), X)
        S.finish()
        stats = dict(issued=dict(S.nissued), waits=S.nwaits)
    return nc, stats


def _ppack(v, n):
    return np.ascontiguousarray(np.asarray(v, np.float32).reshape(n, 128).T)


def _rope_tables(rotate):
    t = np.arange(T)
    row = (t // 64).astype(np.float64)
    col = (t % 64).astype(np.float64)
    out = np.zeros((128, 4, T), np.float32)
    for base, dim in ((0, 128), (2, 64)):
        half = dim // 2
        inv = 10000.0 ** (-np.arange(0, half, 2, dtype=np.float64) / half)
        nf = half // 2
        for m in range(dim):
            pos = row if m < half else col
            i = (m % half) % nf
            ang = pos * inv[i] if rotate else np.zeros(T)
            out[m, base, :] = np.cos(ang)
            out[m, base + 1, :] = np.sin(ang)
        if dim < 128:
            out[dim:, base, :] = 1.0
    return out


def _rot_lhsT(dim):
    half = dim // 2
    nf = half // 2
    m = np.zeros((128, 128), np.float32)
    for o in (0, half):
        for i in range(nf):
            m[o + nf + i, o + i] = -1.0
            m[o + i, o + nf + i] = 1.0
    return m


def _core_inputs(inp, core, nl):
    prompt = core < 4
    p = np.arange(128, dtype=np.float32)
    if prompt:
        x = np.asarray(inp["x_prompt"][4 * core:4 * core + 4], np.float32).reshape(T, D)
        cond = np.asarray(inp["c_ctx"], np.float32)
    else:
        b = core - 4
        x = np.asarray(inp["x_sample"][b], np.float32)
        cond = np.asarray(inp["c"][b], np.float32)
    m = {}
    m["xT"] = np.ascontiguousarray(x.T)
    pvv = np.zeros((nl, 128, PVL), np.float32)
    for l in range(nl):
        for j in range(4):
            pvv[l, :, PV_GN + 16 * j:PV_GN + 16 * (j + 1)] = _ppack(inp["g_norm"][l, j], 16)
        pvv[l, :, PV_BMOD:PV_BMOD + 96] = _ppack(inp["b_mod"][l], 96)
        for d in range(2):
            for j in range(4):
                pvv[l, :, PV_CW + (d * 4 + j) * 8:PV_CW + (d * 4 + j) * 8 + 8] = _ppack(inp["lru_conv_w"][l, d, j], 8)
            pvv[l, :, PV_CB + d * 8:PV_CB + d * 8 + 8] = _ppack(inp["lru_conv_b"][l, d], 8)
            pvv[l, :, PV_BA + d * 8:PV_BA + d * 8 + 8] = _ppack(inp["lru_ba"][l, d], 8)
            pvv[l, :, PV_BX + d * 8:PV_BX + d * 8 + 8] = _ppack(inp["lru_bx"][l, d], 8)
            pvv[l, :, PV_LAM + d * 8:PV_LAM + d * 8 + 8] = _ppack(inp["lru_lam"][l, d], 8)
            if not prompt:
                pvv[l, :, PV_H0 + d * 8:PV_H0 + d * 8 + 8] = _ppack(inp["state_lru"][core - 4, l, d], 8)
            pvv[l, :, PV_DEC + d * 8:PV_DEC + d * 8 + 8] = np.asarray(inp["ret_decay"][l, d], np.float32)[None, :]
        pvv[l, :, PV_GQ:PV_GQ + 4] = _ppack(inp["mla_gq"][l], 4)
        pvv[l, :, PV_GKV:PV_GKV + 2] = _ppack(inp["mla_gkv"][l], 2)
    m["pv"] = pvv
    g = np.zeros((128, GLW), np.float32)
    g[:, GL_COND:GL_COND + 16] = _ppack(cond, 16)
    cross = 0.0 if prompt else 1.0
    g[:, GL_CROSS] = cross
    g[:, GL_LM] = (cross - 1.0) * (-LMV)
    g[:, GL_PIDX] = p
    for par in range(2):
        g[:, GL_POSF + par] = 255 - par * 128 - p
        g[:, GL_POSB + par] = par * 128 + p
    for bi, (c0, c1) in enumerate(EBLK):
        g[:, GL_PSH + bi] = c0 - 1024 - p
    m["gl"] = g
    cst = np.zeros((128, 5, 128), np.float32)
    cst[:, 0, :] = np.eye(128, dtype=np.float32)
    cst[:, 1, :] = 1.0
    cst[:, 2, :] = _rot_lhsT(128)
    cst[:, 3, :] = _rot_lhsT(64)
    cst[:, 4, :] = 1.0 / 128.0
    m["cst"] = cst
    m["cs"] = _rope_tables(rotate=not prompt)
    m["xib"] = np.ascontiguousarray(np.broadcast_to(np.arange(512, dtype=np.float32)[None, :], (128, 512)))
    mqv = np.zeros((4, T), np.float32)
    for gseg in range(4):
        mqv[gseg, gseg * 256:(gseg + 1) * 256] = 1.0
    m["mq"] = mqv
    mkv = np.zeros((4, 1280), np.float32)
    if prompt:
        mkv[:, :] = -BIG
        for gseg in range(4):
            mkv[gseg, 256 + gseg * 256:256 + (gseg + 1) * 256] = 0.0
    m["mk"] = mkv
    if prompt:
        m["cckv"] = np.zeros((nl, 256, 256), np.float32)
        m["ckr"] = np.zeros((nl, 64, 256), np.float32)
        m["sret"] = np.zeros((nl, 2, 8, 128, 128), np.float32)
    else:
        b = core - 4
        m["cckv"] = np.ascontiguousarray(np.asarray(inp["cache_mla_ckv"][b, :nl], np.float32).transpose(0, 2, 1))
        m["ckr"] = np.ascontiguousarray(np.asarray(inp["cache_mla_krope"][b, :nl], np.float32).transpose(0, 2, 1))
        m["sret"] = np.ascontiguousarray(np.asarray(inp["state_ret"][b, :nl], np.float32))
    return m


_WNAMES = ["w_mod", "w_in", "lru_wa", "lru_wx", "mla_wuq", "mla_wukv", "w_br_lru", "w_br_mla", "w_br_ret",
           "w_out", "w_ff1", "w_ff2"]


def run_cores(inp, cores, nl, dbg_names=(), trace=False, stop_after=None):
    nc, stats = build(nl, dbg_names, stop_after)
    shared = {n: np.ascontiguousarray(np.asarray(inp[n], np.float32)[:nl]) for n in _WNAMES}
    in_maps = []
    for c in cores:
        m = _core_inputs(inp, c, nl)
        m.update(shared)
        in_maps.append(m)
    res = run_bass_kernel_spmd(nc, in_maps, core_ids=list(range(len(cores))), trace=trace)
    return res, stats


def kernel(**inputs):
    nl = DEPTH
    res, _ = run_cores(inputs, list(range(8)), nl)
    r = res.results
    y_prompt = np.zeros((16, 256, D), np.float32)
    y_sample = np.zeros((4, T, D), np.float32)
    new_ckv = np.zeros((16, nl, 256, 256), np.float32)
    new_kr = np.zeros((16, nl, 256, 64), np.float32)
    new_lru = np.zeros((16, nl, 2, 1024), np.float32)
    new_ret = np.zeros((16, nl, 2, 8, 128, 128), np.float32)
    for c in range(8):
        o = r[c]
        y = np.asarray(o["yT"]).T
        if c < 4:
            y_prompt[4 * c:4 * c + 4] = y.reshape(4, 256, D)
            ck = np.asarray(o["ckvo"])
            new_ckv[4 * c:4 * c + 4] = ck.reshape(nl, 256, 4, 256).transpose(2, 0, 3, 1)
            kr = np.asarray(o["kro"])
            new_kr[4 * c:4 * c + 4] = kr.reshape(nl, 64, 4, 256).transpose(2, 0, 3, 1)
            lr = np.asarray(o["lruo"]).reshape(128, 4, 2, 8, 4)[:, :nl]
            new_lru[4 * c:4 * c + 4] = lr.transpose(4, 1, 2, 3, 0).reshape(4, nl, 2, 1024)
            rt = np.asarray(o["reto"])
            new_ret[4 * c:4 * c + 4] = rt.transpose(1, 0, 2, 3, 4, 5)
        else:
            y_sample[c - 4] = y
    return (y_prompt, y_sample, new_ckv, new_kr, new_lru, new_ret)
```
